# Optimizing a Trainium2 kernel written in Bass

```python
import math
import jax, jax.numpy as jnp
from jax import lax
import numpy as np

D_MODEL = 1024
BATCH = 8
SEQ = 4096
DEPTH = 4

N_EVEN = (DEPTH + 1) // 2
N_ODD = DEPTH // 2

D_A = D_MODEL // 2
A_CONV = 3
D_B = D_MODEL // 2
POOL_WINDOWS = (2, 4, 8, 16)
POOL_GROUPS = len(POOL_WINDOWS)
POOL_GC = D_B // POOL_GROUPS
EV_IN = 3 * D_A + D_B
EV_MIX = D_A + D_B

D_C = D_MODEL // 2
HEAD_DIM = 64
H_C = D_C // HEAD_DIM
Q_BLOCK = 128
D_D = D_MODEL // 2
D_CONV = 31
OD_IN = 3 * D_C + H_C + 2 * D_D
OD_MIX = D_C + D_D

D_FF = int(math.ceil((8 * D_MODEL / 3) / 256) * 256)

ALPHA = (2.0 * DEPTH) ** 0.25
BETA = (8.0 * DEPTH) ** -0.25
LN_EPS = 1e-5

kernel_name = "hybrid_shortconv_pool_fox_conformer_deepnorm"


def layer_norm(x, g, b):
    x32 = x.astype(jnp.float32)
    mu = jnp.mean(x32, axis=-1, keepdims=True)
    var = jnp.mean(jnp.square(x32 - mu), axis=-1, keepdims=True)
    y = (x32 - mu) * lax.rsqrt(var + LN_EPS)
    return (y * g.astype(jnp.float32) + b.astype(jnp.float32)).astype(x.dtype)


def causal_dwconv(x, w):
    k = w.shape[0]
    return lax.conv_general_dilated(
        x, w[:, None, :].astype(x.dtype), window_strides=(1,), padding=[(k - 1, 0)],
        dimension_numbers=("NWC", "WIO", "NWC"), feature_group_count=x.shape[-1])


def short_gated_conv(p_a, conv_w):
    b_gate = p_a[..., :D_A]
    c_gate = p_a[..., D_A:2 * D_A]
    val = p_a[..., 2 * D_A:]
    return b_gate * causal_dwconv(c_gate * val, conv_w)


def multiscale_pool(u, w_grp, scale):
    bsz, s, _ = u.shape
    u32 = u.astype(jnp.float32).reshape(bsz, s, POOL_GROUPS, POOL_GC)
    cs = jnp.cumsum(u32, axis=1)
    pos = jnp.arange(s)
    pooled = []
    for g, win in enumerate(POOL_WINDOWS):
        c = cs[:, :, g]
        lagged = jnp.pad(c, ((0, 0), (win, 0), (0, 0)))[:, :s]
        cnt = jnp.minimum(pos + 1, win).astype(jnp.float32)[None, :, None]
        pooled.append((c - lagged) / cnt)
    pooled = (jnp.stack(pooled, axis=2) - u32).astype(u.dtype)
    y = jnp.einsum("bsgc,gcd->bsgd", pooled, w_grp).reshape(bsz, s, D_B)
    return y * scale


def forgetting_attention(q, k, v, log_f):
    s_len = q.shape[1]
    cum = jnp.cumsum(log_f.astype(jnp.float32), axis=1).transpose(0, 2, 1)
    scale = 1.0 / math.sqrt(HEAD_DIM)
    outs = []
    for blk in range(s_len // Q_BLOCK):
        q0, q1 = blk * Q_BLOCK, (blk + 1) * Q_BLOCK
        logits = jnp.einsum("bqhd,bkhd->bhqk", q[:, q0:q1], k[:, :q1]).astype(jnp.float32) * scale
        logits = logits + cum[:, :, q0:q1, None] - cum[:, :, None, :q1]
        mask = jnp.arange(q0, q1)[:, None] >= jnp.arange(q1)[None, :]
        logits = jnp.where(mask[None, None], logits, -1e30)
        probs = jax.nn.softmax(logits, axis=-1).astype(v.dtype)
        outs.append(jnp.einsum("bhqk,bkhd->bqhd", probs, v[:, :q1]))
    return jnp.concatenate(outs, axis=1)


def conformer_conv(p_d, dw_w, dw_b, cn_g, cn_b):
    h = p_d[..., :D_D] * jax.nn.sigmoid(p_d[..., D_D:])
    h = causal_dwconv(h, dw_w) + dw_b
    h = layer_norm(h, cn_g, cn_b)
    return jax.nn.silu(h)


def even_mixer(x, w_in, conv_w, pool_w, pool_scale, w_out):
    p = x @ w_in
    y_a = short_gated_conv(p[..., :3 * D_A], conv_w)
    y_b = multiscale_pool(p[..., 3 * D_A:], pool_w, pool_scale)
    return jnp.concatenate([y_a, y_b], axis=-1) @ w_out


def odd_mixer(x, w_in, forget_b, dw_w, dw_b, cn_g, cn_b, w_out):
    bsz, s, _ = x.shape
    p = x @ w_in
    q = p[..., :D_C].reshape(bsz, s, H_C, HEAD_DIM)
    k = p[..., D_C:2 * D_C].reshape(bsz, s, H_C, HEAD_DIM)
    v = p[..., 2 * D_C:3 * D_C].reshape(bsz, s, H_C, HEAD_DIM)
    log_f = jax.nn.log_sigmoid((p[..., 3 * D_C:3 * D_C + H_C] + forget_b).astype(jnp.float32))
    y_c = forgetting_attention(q, k, v, log_f).reshape(bsz, s, D_C)
    y_d = conformer_conv(p[..., 3 * D_C + H_C:], dw_w, dw_b, cn_g, cn_b)
    return jnp.concatenate([y_c, y_d], axis=-1) @ w_out


def swiglu(x, w_in, w_out):
    h = x @ w_in
    return (jax.nn.silu(h[..., :D_FF]) * h[..., D_FF:]) @ w_out


def setup_inputs(seed: int = 0) -> dict:
    key = jax.random.key(seed)
    ks = jax.random.split(key, 24)
    f32 = jnp.float32
    nrm = lambda k, shape, s: jax.random.normal(k, shape, f32) * s
    x = jax.random.normal(ks[0], (BATCH, SEQ, D_MODEL), f32)
    ev_w_in = nrm(ks[1], (N_EVEN, D_MODEL, EV_IN), D_MODEL ** -0.5)
    ev_conv_w = nrm(ks[2], (N_EVEN, A_CONV, D_A), A_CONV ** -0.5)
    ev_pool_w = nrm(ks[3], (N_EVEN, POOL_GROUPS, POOL_GC, POOL_GC), POOL_GC ** -0.5)
    ev_pool_scale = 1.0 + nrm(ks[4], (N_EVEN, D_B), 0.1)
    ev_w_out = nrm(ks[5], (N_EVEN, EV_MIX, D_MODEL), EV_MIX ** -0.5 * BETA)
    col_scale = jnp.ones((OD_IN,), f32).at[2 * D_C:3 * D_C].set(BETA)
    od_w_in = nrm(ks[6], (N_ODD, D_MODEL, OD_IN), D_MODEL ** -0.5) * col_scale
    od_forget_b = jax.random.uniform(ks[7], (N_ODD, H_C), f32, 1.0, 4.0)
    od_dw_w = nrm(ks[8], (N_ODD, D_CONV, D_D), D_CONV ** -0.5)
    od_dw_b = nrm(ks[9], (N_ODD, D_D), 0.02)
    od_cn_g = 1.0 + nrm(ks[10], (N_ODD, D_D), 0.05)
    od_cn_b = nrm(ks[11], (N_ODD, D_D), 0.02)
    od_w_out = nrm(ks[12], (N_ODD, OD_MIX, D_MODEL), OD_MIX ** -0.5 * BETA)
    ln_mix_g = 1.0 + nrm(ks[13], (DEPTH, D_MODEL), 0.05)
    ln_mix_b = nrm(ks[14], (DEPTH, D_MODEL), 0.02)
    ln_ffn_g = 1.0 + nrm(ks[15], (DEPTH, D_MODEL), 0.05)
    ln_ffn_b = nrm(ks[16], (DEPTH, D_MODEL), 0.02)
    ffn_w_in = nrm(ks[17], (DEPTH, D_MODEL, 2 * D_FF), D_MODEL ** -0.5)
    ffn_w_out = nrm(ks[18], (DEPTH, D_FF, D_MODEL), D_FF ** -0.5 * BETA)
    return {"x": x, "ev_w_in": ev_w_in, "ev_conv_w": ev_conv_w, "ev_pool_w": ev_pool_w,
            "ev_pool_scale": ev_pool_scale, "ev_w_out": ev_w_out, "od_w_in": od_w_in,
            "od_forget_b": od_forget_b, "od_dw_w": od_dw_w, "od_dw_b": od_dw_b,
            "od_cn_g": od_cn_g, "od_cn_b": od_cn_b, "od_w_out": od_w_out,
            "ln_mix_g": ln_mix_g, "ln_mix_b": ln_mix_b, "ln_ffn_g": ln_ffn_g,
            "ln_ffn_b": ln_ffn_b, "ffn_w_in": ffn_w_in, "ffn_w_out": ffn_w_out}


def reference(x, ev_w_in, ev_conv_w, ev_pool_w, ev_pool_scale, ev_w_out, od_w_in,
              od_forget_b, od_dw_w, od_dw_b, od_cn_g, od_cn_b, od_w_out,
              ln_mix_g, ln_mix_b, ln_ffn_g, ln_ffn_b, ffn_w_in, ffn_w_out):
    for layer in range(DEPTH):
        i = layer // 2
        if layer % 2 == 0:
            mix = even_mixer(x, ev_w_in[i], ev_conv_w[i], ev_pool_w[i], ev_pool_scale[i], ev_w_out[i])
        else:
            mix = odd_mixer(x, od_w_in[i], od_forget_b[i], od_dw_w[i], od_dw_b[i],
                            od_cn_g[i], od_cn_b[i], od_w_out[i])
        x = layer_norm(ALPHA * x + mix, ln_mix_g[layer], ln_mix_b[layer])
        x = layer_norm(ALPHA * x + swiglu(x, ffn_w_in[layer], ffn_w_out[layer]),
                       ln_ffn_g[layer], ln_ffn_b[layer])
    return x
```

```python
import numpy as np
import ml_dtypes
import concourse.bass as bass
import concourse.mybir as mybir
from concourse.bass_utils import run_bass_kernel_spmd

F32 = mybir.dt.float32
BF16 = mybir.dt.bfloat16
U8 = mybir.dt.uint8
AF = mybir.ActivationFunctionType
ALU = mybir.AluOpType

S = 4096
D = 1024
DEPTH = 4
DFF = 2816
NJ = DFF // 128
OD_IN = 2568
ALPHA = (2.0 * DEPTH) ** 0.25
LN_EPS = 1e-5
NPAR = 324
NEG = -30000.0


class Op:
    __slots__ = ("eng", "fn", "dma", "sem", "deps", "sig", "cnt", "gend", "grp")


class Grp:
    def __init__(self, sem):
        self.sem = sem
        self.ops = []


class Prog:
    COMPUTE = ("pe", "act", "dve", "pool")

    def __init__(self):
        self.ops = []
        self.lw = {}
        self.rd = {}
        self.last_eng = {}
        self.last_sem = {}
        self.bar_deps = []
        self.bar_pending = set()

    def add(self, eng, fn, reads=(), writes=(), grp=None, sem=None):
        op = Op()
        op.eng = eng
        op.fn = fn
        if sem is not None and grp is None:
            grp = Grp(sem)
        op.dma = grp is not None
        op.grp = grp
        op.sig = False
        op.cnt = 0
        deps = {}
        for k in reads:
            w = self.lw.get(k)
            if w is not None:
                deps[w] = True
        for k in writes:
            w = self.lw.get(k)
            if w is not None:
                deps.setdefault(w, False)
            for r in self.rd.get(k, ()):
                deps.setdefault(r, False)
        if eng in self.bar_pending:
            for d in self.bar_deps:
                deps[d] = True
            self.bar_pending.discard(eng)
        deps.pop(op, None)
        keep = []
        for d, raw in deps.items():
            if d.dma and op.dma and d.grp is grp:
                continue
            if (not d.dma) and (not op.dma) and d.eng == eng:
                if eng == "pe":
                    continue
            keep.append(d)
            if not d.dma:
                d.sig = True
        op.deps = keep
        for k in reads:
            self.rd.setdefault(k, []).append(op)
        for k in writes:
            self.lw[k] = op
            self.rd[k] = []
        self.ops.append(op)
        if op.dma:
            grp.ops.append(op)
            self.last_sem[grp.sem] = op
        else:
            self.last_eng[eng] = op
        return op

    def barrier(self):
        self.bar_deps = list(self.last_eng.values()) + list(self.last_sem.values())
        self.bar_pending = set(["pe", "act", "dve", "pool", "sp"])

    def finalize(self):
        cnt = {}
        for op in self.ops:
            if op.dma:
                s = op.grp.sem
                cnt[s] = cnt.get(s, 0) + 16
                op.cnt = cnt[s]
            elif op.sig:
                cnt[op.eng] = cnt.get(op.eng, 0) + 1
                op.cnt = cnt[op.eng]
        self.sem_names = sorted(cnt.keys())
        self.final_counts = cnt
        for op in self.ops:
            w = {}
            for d in op.deps:
                if d.dma:
                    s = d.grp.sem
                    v = d.grp.ops[-1].cnt
                else:
                    s = d.eng
                    v = d.cnt
                if w.get(s, 0) < v:
                    w[s] = v
            op.deps = w

    def emit(self, engname, eng, sems):
        known = {}
        n = 0
        for op in self.ops:
            if op.eng != engname:
                continue
            for s, v in op.deps.items():
                if known.get(s, 0) < v:
                    eng.wait_ge(sems[s], v)
                    known[s] = v
            if op.fn is None:
                continue
            ins = op.fn(eng)
            n += 1
            if op.dma:
                ins.then_inc(sems[op.grp.sem], 16)
            elif op.sig:
                ins.then_inc(sems[engname], 1)
        return n


class Rot:
    def __init__(self, items):
        self.items = list(items)
        self.i = 0

    def next(self):
        v = self.items[self.i % len(self.items)]
        self.i += 1
        return v


class Arena:
    def __init__(self, ap, size):
        self.ap = ap
        self.size = size
        self.off = 0
        self.mark = 0

    def alloc(self, shape, dt, parts=128):
        esz = 4 if dt == F32 else 2
        n = 1
        for s_ in shape[1:]:
            n *= s_
        nbytes = n * esz
        off = (self.off + 63) // 64 * 64
        assert off + nbytes <= self.size, f"arena overflow {off + nbytes} > {self.size}"
        self.off = off + nbytes
        v = self.ap[0:shape[0], off:off + nbytes].bitcast(dt)
        if len(shape) == 3:
            v = v.rearrange("p (a b) -> p a b", a=shape[1])
        elif len(shape) == 4:
            v = v.rearrange("p (a b c) -> p a b c", a=shape[1], b=shape[2])
        return v

    def set_mark(self):
        self.mark = self.off

    def reset(self):
        self.off = self.mark


def build_nc(nphases=8, debug=False):
    nc = bass.Bass("TRN2", target_bir_lowering=False)
    P = Prog()

    def dram_in(name, shape, dt=F32):
        return nc.dram_tensor(name, list(shape), dt, kind="ExternalInput").ap()

    x_in = dram_in("x", [S, D])
    ev_w_in = dram_in("ev_w_in", [2, D, 2048])
    ev_pool_w = dram_in("ev_pool_w", [2, 4, 128, 128])
    ev_w_out = dram_in("ev_w_out", [2, D, D])
    od_w_in = dram_in("od_w_in", [2, D, OD_IN])
    od_w_out = dram_in("od_w_out", [2, D, D])
    ln_mix_g = dram_in("ln_mix_g", [4, D])
    ln_mix_b = dram_in("ln_mix_b", [4, D])
    ln_ffn_g = dram_in("ln_ffn_g", [4, D])
    ln_ffn_b = dram_in("ln_ffn_b", [4, D])
    ffn_w_in = dram_in("ffn_w_in", [4, D, 2 * DFF])
    ffn_w_out = dram_in("ffn_w_out", [4, DFF, D])
    cpar_in = dram_in("cpar", [128, NPAR])
    cbf_in = dram_in("cbf", [128, 256], BF16)
    y_out = nc.dram_tensor("y", [S, D], F32, kind="ExternalOutput").ap()

    def scratch(name, shape, dt):
        if debug and name in ("QT", "KT", "Vd", "QX", "KX", "YC", "YD"):
            return nc.dram_tensor(name, list(shape), dt, kind="ExternalOutput").ap()
        return nc.dram_tensor(name, list(shape), dt).ap()

    X = [scratch(f"Xs{i}", [S, D], F32) for i in range(2)]
    XB = [scratch(f"XBs{i}", [S, D], BF16) for i in range(2)]
    Wevin = [scratch(f"Wevin{i}", [D, 2048], BF16) for i in range(2)]
    Wevout = [scratch(f"Wevout{i}", [D, D], BF16) for i in range(2)]
    Wpool = [scratch(f"Wpool{i}", [4, 128, 128], BF16) for i in range(2)]
    Wodin = [scratch(f"Wodin{i}", [D, OD_IN], BF16) for i in range(2)]
    Wodout = [scratch(f"Wodout{i}", [D, D], BF16) for i in range(2)]
    W1b = [scratch(f"W1b{l}", [D, 2 * DFF], BF16) for l in range(4)]
    W2b = [scratch(f"W2b{l}", [DFF, D], BF16) for l in range(4)]
    QT = scratch("QT", [512, S], BF16)
    KT = scratch("KT", [512, S], BF16)
    Vd = scratch("Vd", [S, 512], BF16)
    QX = scratch("QX", [8, 4, S], BF16)
    KX = scratch("KX", [8, 4, S], BF16)
    YC = scratch("YC", [512, S], BF16)
    YD = scratch("YD", [512, S], BF16)

    phases = [("E", 0, 0), ("F", 0, 0), ("O", 1, 0), ("F", 1, 0),
              ("E", 2, 1), ("F", 2, 1), ("O", 3, 1), ("F", 3, 1)][:nphases]

    ARENA = 212480
    import contextlib
    with contextlib.ExitStack() as es:
        arena_t = es.enter_context(nc.sbuf_tensor("arena", [128, ARENA], U8))
        ps = es.enter_context(nc.psum_tensor("ps", [128, 8, 512], F32))
        A = Arena(arena_t, ARENA)

        cbf = A.alloc([128, 256], BF16)
        ident = cbf[:, 0:128]
        tri = cbf[:, 128:256]
        cpar = A.alloc([128, NPAR], F32)
        nfb = A.alloc([128, 2], F32)
        lng2 = [A.alloc([128, D], F32) for _ in range(2)]
        lnb2 = [A.alloc([128, D], F32) for _ in range(2)]
        xres = [A.alloc([128, D], F32) for _ in range(4)]
        xbo = [A.alloc([128, D], BF16) for _ in range(2)]
        stt_ = [A.alloc([128, 2, 6], F32) for _ in range(4)]
        mvs = [A.alloc([128, 8], F32) for _ in range(4)]
        xTg = [A.alloc([128, 8, 512], BF16) for _ in range(2)]
        A.set_mark()

        P.add("sp", lambda e: e.dma_start(out=cbf, in_=cbf_in), writes=["cbf"], sem="c0")
        P.add("sp", lambda e: e.dma_start(out=cpar, in_=cpar_in), writes=["cpar"], sem="c1")
        P.add("dve", lambda e: e.tensor_scalar(out=nfb[:, 0:2], in0=cpar[:, 304:306], scalar1=-1.0, scalar2=None,
                                               op0=ALU.mult), reads=["cpar"], writes=["nfb"])

        bgq = []
        bg_mode = [False]
        bg_tag = [0]

        bg_pace = [None]

        def cast(out_ap, in_ap, key, sem):
            def emit():
                wk = [key] + ([bg_pace[0]] if bg_pace[0] is not None else [])
                P.add("pool", lambda e: e.dma_start(out=out_ap, in_=in_ap), writes=wk, grp=sem)
            if bg_mode[0]:
                bgq.append((bg_tag[0], emit))
            else:
                emit()

        def bgcast(n, pace=None):
            bg_pace[0] = pace
            for _ in range(n):
                if bgq:
                    bgq.pop(0)[1]()
            bg_pace[0] = None

        def bg_flush(ptag):
            while bgq and bgq[0][0] <= ptag:
                bgq.pop(0)[1]()

        def cast_x(t):
            gxb = Grp("k_xb%d" % t)
            P.add("pool", lambda e, t=t: e.dma_start(out=XB[0][t * 512:(t + 1) * 512, :], in_=x_in[t * 512:(t + 1) * 512, :]),
                  writes=[("XB", 0, 4 * t + q) for q in range(4)], grp=gxb)

        def cast_even(i):
            g = Grp("k_ev%d" % i)
            for q in (1, 2, 0, 3):
                cast(Wevin[i][:, q * 512:(q + 1) * 512], ev_w_in[i, :, q * 512:(q + 1) * 512], ("Wevin", i, q), Grp("k_ev%d_%d" % (i, q)))
            for r in range(2):
                cast(Wevout[i][r * 512:(r + 1) * 512, :], ev_w_out[i, r * 512:(r + 1) * 512, :], ("Wevout", i), g)
            cast(Wpool[i].rearrange("g c d -> (g c) d"), ev_pool_w[i].rearrange("g c d -> (g c) d"), ("Wpool", i), g)

        def cast_odd(i):
            g = Grp("k_od%d" % i)
            for r in range(4):
                cast(Wodin[i][r * 256:(r + 1) * 256, :], od_w_in[i, r * 256:(r + 1) * 256, :], ("Wodin", i), g)
            for r in range(2):
                cast(Wodout[i][r * 512:(r + 1) * 512, :], od_w_out[i, r * 512:(r + 1) * 512, :], ("Wodout", i), g)

        def cast_ffn(l):
            g = Grp("k_f%d" % l)
            for r in range(8):
                cast(W1b[l][r * 128:(r + 1) * 128, :], ffn_w_in[l, r * 128:(r + 1) * 128, :], ("W1b", l), g)
            for r in range(4):
                cast(W2b[l][r * 704:(r + 1) * 704, :], ffn_w_out[l, r * 704:(r + 1) * 704, :], ("W2b", l), g)

        def cast_phase(p):
            if p >= len(phases):
                return
            bg_tag[0] = p
            kind, L, i = phases[p]
            if kind == "E":
                cast_even(i)
            elif kind == "O":
                cast_odd(i)
            else:
                cast_ffn(L)

        cast_x(0)
        cast_phase(0)
        cast_x(1)
        cast_x(2)
        bg_mode[0] = True
        cast_phase(1)

        pending_ln = []
        psA = Rot([0, 1, 2, 3])
        psB = Rot([4, 5, 6, 7])
        psC = Rot([4, 5])
        lnslot = Rot([0, 1, 2])
        xboslot = Rot([0, 1])

        def mm_group(out_ap, pairs, reads, writes):
            def fn(pe, pairs=pairs, out_ap=out_ap):
                n = len(pairs)
                ins = None
                for q, (l, r) in enumerate(pairs):
                    ins = pe.matmul(out_ap, lhsT=l, rhs=r, start=(q == 0), stop=(q == n - 1))
                return ins
            return P.add("pe", fn, reads=reads, writes=writes)

        class LNStage:
            def __init__(self, Xsrc, xkey, Xdst, dkey, XBdst, bkey, g_ap, b_ap, final, par):
                self.Xsrc, self.xkey, self.Xdst, self.dkey = Xsrc, xkey, Xdst, dkey
                self.XBdst, self.bkey, self.final = XBdst, bkey, final
                self.pref = {}
                self.pipe = []
                self.psB = psB
                lng, lnb = lng2[par], lnb2[par]
                self.lng, self.lnb = lng, lnb
                self.gk, self.bk = ("lng", par), ("lnb", par)
                P.add("sp", lambda e: e.dma_start(out=lng, in_=g_ap.to_broadcast([128, D])), writes=[self.gk], sem="l_g%d" % par)
                P.add("sp", lambda e: e.dma_start(out=lnb, in_=b_ap.to_broadcast([128, D])), writes=[self.bk], sem="l_b%d" % par)

            def prefetch(self, st):
                if st in self.pref or st >= S // 128:
                    return
                slot = st % 4
                self.pref[st] = slot
                xr = xres[slot]
                src = self.Xsrc[st * 128:(st + 1) * 128, :]
                P.add("sp", lambda e: e.dma_start(out=xr, in_=src), reads=[(self.xkey, st)],
                      writes=[("xres", slot, 0), ("xres", slot, 1)], sem="l_x%d" % slot)

            def tick(self):
                ready3 = [ent for ent in self.pipe if ent[0] is None and ent[1] is not None]
                for ent in self.pipe:
                    if ent[0] is not None:
                        ent[0]()
                        ent[0] = None
                for ent in ready3:
                    ent[1]()
                    ent[1] = None
                self.pipe = [e_ for e_ in self.pipe if e_[1] is not None]

            def flush(self):
                while self.pipe:
                    self.tick()

            def run(self, st, banks):
                self.prefetch(st)
                slot = self.pref.pop(st)
                xr = xres[slot]
                sa = stt_[slot]
                mv = mvs[slot]
                xk = [("xres", slot, 0), ("xres", slot, 1)]
                for h in range(2):
                    P.add("dve", lambda e, h=h: e.scalar_tensor_tensor(
                        out=xr[:, h * 512:(h + 1) * 512], in0=xr[:, h * 512:(h + 1) * 512], scalar=ALPHA,
                        in1=ps[:, banks[h], :], op0=ALU.mult, op1=ALU.add),
                        reads=[("xres", slot, h), ("ps", banks[h])], writes=[("xres", slot, h)])
                    P.add("dve", lambda e, h=h: e.bn_stats(out=sa[:, h, :], in_=xr[:, h * 512:(h + 1) * 512]),
                          reads=[("xres", slot, h)], writes=[("st", slot, h)])
                P.add("dve", lambda e: e.bn_aggr(out=mv[:, 0:2], in_=sa), reads=[("st", slot, 0), ("st", slot, 1)],
                      writes=[("mv", slot, 0)])
                self.prefetch(st + 1)

                def stage2():
                    P.add("act", lambda e: e.activation(out=mv[:, 2:3], in_=mv[:, 1:2], func=AF.Sqrt, bias=LN_EPS, scale=1.0),
                          reads=[("mv", slot, 0)], writes=[("mv", slot, 1)])
                    P.add("dve", lambda e: e.reciprocal(out=mv[:, 3:4], in_=mv[:, 2:3]), reads=[("mv", slot, 1)],
                          writes=[("mv", slot, 2)])
                    P.add("dve", lambda e: e.tensor_scalar(out=xr, in0=xr, scalar1=mv[:, 0:1], scalar2=mv[:, 3:4],
                                                           op0=ALU.subtract, op1=ALU.mult),
                          reads=xk + [("mv", slot, 0), ("mv", slot, 2)], writes=xk)
                    P.add("pool", lambda e: e.tensor_tensor(out=xr, in0=xr, in1=self.lng, op=ALU.mult), reads=xk + [self.gk], writes=xk)
                    P.add("pool", lambda e: e.tensor_tensor(out=xr, in0=xr, in1=self.lnb, op=ALU.add), reads=xk + [self.bk], writes=xk)

                def stage3():
                    dst = self.Xdst[st * 128:(st + 1) * 128, :]
                    if not self.final:
                        bs = st % 2
                        xb_ = xbo[bs]
                        P.add("act", lambda e: e.activation(out=xb_, in_=xr, func=AF.Copy), reads=xk, writes=[("xbo", bs)])
                        P.add("sp", lambda e: e.dma_start(out=dst, in_=xr), reads=xk, writes=[(self.dkey, st)], sem="s_x%d" % slot)
                        bdst = self.XBdst[st * 128:(st + 1) * 128, :]
                        P.add("sp", lambda e: e.dma_start(out=bdst, in_=xb_), reads=[("xbo", bs)],
                              writes=[self.bkey + (st,)], sem="s_b%d" % bs)
                    else:
                        P.add("sp", lambda e: e.dma_start(out=dst, in_=xr), reads=xk, writes=[(self.dkey, st)], sem="s_x%d" % slot)

                self.pipe.append([stage2, stage3])

            def proj(self, st, nk, act_fn, w_fn, reads):
                self.tick()
                banks = [self.psB.next(), self.psB.next()]
                for h in range(2):
                    mm_group(ps[:, banks[h], :], [(act_fn(kc), w_fn(kc, h)) for kc in range(nk)],
                             reads=reads, writes=[("ps", banks[h])])
                self.run(st, banks)

        def make_ln(p, g_all, b_all, L):
            final = (p == len(phases) - 1)
            if p == 0:
                Xsrc, xkey = x_in, "x_in"
            else:
                Xsrc, xkey = X[p % 2], ("X", p % 2)
            if final:
                Xdst, dkey = y_out, "y"
            else:
                Xdst, dkey = X[(p + 1) % 2], ("X", (p + 1) % 2)
            return LNStage(Xsrc, xkey, Xdst, dkey, XB[(p + 1) % 2], ("XB", (p + 1) % 2), g_all[L:L + 1, :], b_all[L:L + 1, :], final, p % 2)

        xt_done = set()

        def load_xT(p, t, dst, slot, q="sp"):
            for _ in load_xT_iter(p, t, dst, slot, q):
                pass

        def load_xT_iter(p, t, dst, slot, q="sp"):
            if p >= len(phases) or (p, t) in xt_done:
                return
            xt_done.add((p, t))
            g = Grp("xT%d" % slot)
            for kc in range(8):
                yield
                P.add(q, lambda e, kc=kc: e.dma_start_transpose(
                    out=dst[:, kc, :], in_=XB[p % 2][t * 512:(t + 1) * 512, kc * 128:(kc + 1) * 128]),
                    reads=[("XB", p % 2, 4 * t + q) for q in range(4)], writes=[("xT", slot)], grp=g)

        def phase_even2(p, L, i):
            A.reset()
            ln = make_ln(p, ln_mix_g, ln_mix_b, L)
            psA5 = Rot([0, 1, 2, 3, 7])
            ln.psB = Rot([4, 5, 6])
            Win = A.alloc([128, 8, 2048], BF16)
            Wout = A.alloc([128, 8, D], BF16)
            poolw = A.alloc([128, 4, 128], BF16)
            xT = xTg
            cv = [A.alloc([128, 4, 514], F32) for _ in range(2)]
            ub = [A.alloc([128, 4, 528], F32) for _ in range(2)]
            ctmp = [A.alloc([128, 512], F32) for _ in range(2)]
            acc = [A.alloc([128, 4, 512], F32) for _ in range(2)]
            tA = [A.alloc([128, 528], F32) for _ in range(2)]
            tB = [A.alloc([128, 528], F32) for _ in range(2)]
            fx = A.alloc([128, 16], F32)
            pooled = [A.alloc([128, 4, 512], BF16) for _ in range(2)]
            mixT = [A.alloc([128, 8, 512], BF16) for _ in range(2)]
            pb = i * 16

            load_xT(p, 0, xT[0], 0)
            for q in (1, 2, 0, 3):
                P.add("sp", lambda e, q=q: e.dma_start(
                    out=Win[:, :, q * 512:(q + 1) * 512],
                    in_=Wevin[i][:, q * 512:(q + 1) * 512].rearrange("(kc p) n -> p kc n", p=128)),
                    reads=[("Wevin", i, q)], writes=[("Win", q)], sem="w_a%d" % q)
            P.add("sp", lambda e: e.dma_start(out=Wout, in_=Wevout[i].rearrange("(kc p) n -> p kc n", p=128)),
                  reads=[("Wevout", i)], writes=["Wout"], sem="w_b")
            P.add("sp", lambda e: e.dma_start(out=poolw, in_=Wpool[i].rearrange("g c d -> c g d")),
                  reads=[("Wpool", i)], writes=["poolw"], sem="w_c")
            P.add("dve", lambda e: e.memset(cv[0][:, :, 0:2], 0.0), writes=[("cvh", 0, j) for j in range(4)])
            P.add("dve", lambda e: e.memset(ub[0][:, :, 0:16], 0.0), writes=[("ubh", 0, j) for j in range(4)])

            NT = S // 512
            yield

            def partA(t):
                cur, prev = t % 2, 1 - (t % 2)
                if t + 1 < NT:
                    load_xT(p, t + 1, xT[prev], prev)

                def inproj(col):
                    b = psA5.next()
                    mm_group(ps[:, b, :], [(Win[:, kc, col:col + 128], xT[cur][:, kc, :]) for kc in range(8)],
                             reads=[("Win", col // 512), ("xT", cur)], writes=[("ps", b)])
                    return b

                for j in range(4):
                    b = inproj(512 + j * 128)
                    ct = ctmp[j % 2]
                    P.add("act", lambda e, b=b, ct=ct: e.activation(out=ct, in_=ps[:, b, :], func=AF.Copy),
                          reads=[("ps", b)], writes=[("ctmp", j % 2)])
                    yield
                    b = inproj(1024 + j * 128)
                    P.add("dve", lambda e, b=b, ct=ct, j=j: e.tensor_tensor(out=cv[cur][:, j, 2:514], in0=ct, in1=ps[:, b, :],
                                                                          op=ALU.mult),
                          reads=[("ctmp", j % 2), ("ps", b)], writes=[("cvm", cur, j)])
                    if t > 0:
                        P.add("act", lambda e, j=j: e.activation(out=cv[cur][:, j, 0:2], in_=cv[prev][:, j, 512:514], func=AF.Copy),
                              reads=[("cvm", prev, j)], writes=[("cvh", cur, j)])
                    w0 = cpar[:, pb + j * 3 + 0:pb + j * 3 + 1]
                    w1 = cpar[:, pb + j * 3 + 1:pb + j * 3 + 2]
                    w2 = cpar[:, pb + j * 3 + 2:pb + j * 3 + 3]
                    ac = acc[cur][:, j, :]
                    P.add("act", lambda e, j=j, ac=ac, w2=w2: e.activation(out=ac, in_=cv[cur][:, j, 2:514], func=AF.Identity, scale=w2),
                          reads=[("cvm", cur, j), "cpar"], writes=[("acc", cur, j)])
                    P.add("dve", lambda e, j=j, ac=ac, w1=w1: e.scalar_tensor_tensor(
                        out=ac, in0=cv[cur][:, j, 1:513], scalar=w1, in1=ac, op0=ALU.mult, op1=ALU.add),
                        reads=[("cvm", cur, j), ("cvh", cur, j), ("acc", cur, j), "cpar"], writes=[("acc", cur, j)])
                    P.add("dve", lambda e, j=j, ac=ac, w0=w0: e.scalar_tensor_tensor(
                        out=ac, in0=cv[cur][:, j, 0:512], scalar=w0, in1=ac, op0=ALU.mult, op1=ALU.add),
                        reads=[("cvm", cur, j), ("cvh", cur, j), ("acc", cur, j), "cpar"], writes=[("acc", cur, j)])
                    yield
                for j in range(4):
                    b = inproj(j * 128)
                    P.add("dve", lambda e, j=j, b=b: e.tensor_tensor(out=mixT[cur][:, j, :], in0=acc[cur][:, j, :], in1=ps[:, b, :],
                                                                     op=ALU.mult),
                          reads=[("acc", cur, j), ("ps", b)], writes=[("mixT", cur, j)])
                    yield
                for gq in (3, 2, 1, 0):
                    win = 2 ** (gq + 1)
                    b = inproj(1536 + gq * 128)
                    U = ub[cur][:, gq, :]
                    P.add("act", lambda e, b=b, U=U: e.activation(out=U[:, 16:528], in_=ps[:, b, :], func=AF.Copy),
                          reads=[("ps", b)], writes=[("ubm", cur, gq)])
                    if t > 0:
                        P.add("act", lambda e, gq=gq, U=U: e.activation(out=U[:, 0:16], in_=ub[prev][:, gq, 512:528], func=AF.Copy),
                              reads=[("ubm", prev, gq)], writes=[("ubh", cur, gq)])
                    src = U
                    srck = [("ubm", cur, gq), ("ubh", cur, gq)]
                    for k in range(gq + 1):
                        d = 2 ** k
                        lo = 2 * d - 1
                        dst = (tA if k % 2 == 0 else tB)[gq % 2]
                        dk = [("tAB", k % 2, gq % 2)]
                        P.add("pool", lambda e, dst=dst, src=src, lo=lo, d=d: e.tensor_tensor(
                            out=dst[:, lo:528], in0=src[:, lo:528], in1=src[:, lo - d:528 - d], op=ALU.add),
                            reads=srck, writes=dk)
                        src, srck = dst, dk
                    pl = pooled[cur][:, gq, :]
                    pk = ("pooled", cur, gq)
                    P.add("dve", lambda e, pl=pl, src=src, U=U, win=win: e.scalar_tensor_tensor(
                        out=pl, in0=src[:, 16:528], scalar=1.0 / win, in1=U[:, 16:528], op0=ALU.mult, op1=ALU.subtract),
                        reads=srck + [("ubm", cur, gq)], writes=[pk])
                    if t == 0:
                        n1 = win - 1
                        P.add("dve", lambda e, src=src, n1=n1: e.tensor_tensor(
                            out=fx[:, 0:n1], in0=src[:, 16:16 + n1], in1=cpar[:, 306:306 + n1], op=ALU.mult),
                            reads=srck + ["cpar"], writes=["fx"])
                        P.add("dve", lambda e, pl=pl, U=U, n1=n1: e.tensor_tensor(
                            out=pl[:, 0:n1], in0=fx[:, 0:n1], in1=U[:, 16:16 + n1], op=ALU.subtract),
                            reads=["fx", ("ubm", cur, gq), pk], writes=[pk])
                    yield

            def partB(t):
                cur = t % 2
                for gq in range(4):
                    b = psA5.next()
                    mm_group(ps[:, b, :], [(poolw[:, gq, :], pooled[cur][:, gq, :])], reads=["poolw", ("pooled", cur, gq)],
                             writes=[("ps", b)])
                    sc = cpar[:, pb + 12 + gq:pb + 13 + gq]
                    P.add("act", lambda e, b=b, gq=gq, sc=sc: e.activation(out=mixT[cur][:, 4 + gq, :], in_=ps[:, b, :],
                                                                           func=AF.Identity, scale=sc),
                          reads=[("ps", b), "cpar"], writes=[("mixT", cur, 4 + gq)])
                yield
                for s_ in range(4):
                    st = t * 4 + s_
                    ln.proj(st, 8, lambda kc, s_=s_: mixT[cur][:, kc, s_ * 128:(s_ + 1) * 128],
                            lambda kc, h: Wout[:, kc, h * 512:(h + 1) * 512],
                            reads=[("mixT", cur, j) for j in range(8)] + ["Wout"])
                    yield

            def steps(g, n):
                if g is None:
                    return
                for _ in range(n):
                    next(g, None)

            ln.prefetch(0)
            for _ in partA(0):
                pass
            for t in range(NT):
                if p == 0 and t + 3 < NT:
                    cast_x(t + 3)
                bgcast(2)
                gA = partA(t + 1) if t + 1 < NT else None
                gB = partB(t)
                if t == NT - 1:
                    load_xT(p + 1, 0, xT[0], 0)
                steps(gA, 4)
                steps(gB, 1)
                steps(gA, 4)
                steps(gB, 1)
                steps(gA, 4)
                steps(gB, 1)
                steps(gA, 4)
                steps(gB, 2)
                for g_ in (gA, gB):
                    if g_ is not None:
                        for _ in g_:
                            pass
            pending_ln.append(ln)

        w1cnt = [0]

        def phase_ffn(p, L):
            A.reset()
            ln = make_ln(p, ln_ffn_g, ln_ffn_b, L)
            NR = 6
            W2 = A.alloc([128, NJ, D], BF16)
            w1r = [A.alloc([128, 8, 2, 128], BF16) for _ in range(NR)]
            xT = xTg
            hT = A.alloc([128, NJ, 1024], BF16)
            sg = [A.alloc([128, 512], F32) for _ in range(2)]
            sgslot = Rot([0, 1])

            NTT = S // 1024
            pieces = [(T, j) for T in range(NTT) for j in range(NJ)]

            def load_piece(n):
                if n >= len(pieces):
                    return
                T, j = pieces[n]
                slot = n % NR
                gp = Grp("w1r%d" % slot)
                for gu in range(2):
                    c0 = gu * DFF + j * 128
                    P.add("sp", lambda e, gu=gu, c0=c0: e.dma_start(
                        out=w1r[slot][:, :, gu, :], in_=W1b[L][:, c0:c0 + 128].rearrange("(kc p) n -> p kc n", p=128)),
                        reads=[("W1b", L)], writes=[("w1r", slot)], grp=gp)

            load_xT(p, 0, xT[0], 0)
            load_piece(0)
            load_xT(p, 1, xT[1], 1)
            for n in range(1, NR - 1):
                load_piece(n)
            def load_W2():
                g = Grp("w_a")
                for q in range(2):
                    P.add("sp", lambda e, q=q: e.dma_start(
                        out=W2[:, q * 11:(q + 1) * 11, :],
                        in_=W2b[L][q * 11 * 128:(q + 1) * 11 * 128, :].rearrange("(j p) n -> p j n", p=128)),
                        reads=[("W2b", L)], writes=["W2"], grp=g)
            ln.prefetch(0)
            yield

            n = 0
            for T in range(NTT):
                for j in range(NJ):
                    if j % 8 == 4:
                        bgcast(1, pace=("pace", p, n - 1))
                    load_piece(n + NR - 1)
                    if n == 6:
                        load_W2()
                    slot = n % NR
                    for sub in range(2):
                        bg, bu = psA.next(), psA.next()
                        mm_group(ps[:, bg, :], [(w1r[slot][:, kc, 0, :], xT[sub][:, kc, :]) for kc in range(8)],
                                 reads=[("w1r", slot), ("xT", sub)], writes=[("ps", bg)])
                        mm_group(ps[:, bu, :], [(w1r[slot][:, kc, 1, :], xT[sub][:, kc, :]) for kc in range(8)],
                                 reads=[("w1r", slot), ("xT", sub)], writes=[("ps", bu)])
                        ss = sgslot.next()
                        sgt = sg[ss]
                        P.add("act", lambda e, bg=bg, sgt=sgt: e.activation(out=sgt, in_=ps[:, bg, :], func=AF.Silu),
                              reads=[("ps", bg)], writes=[("sg", ss)])
                        P.add("dve", lambda e, bu=bu, sgt=sgt, j=j, sub=sub: e.tensor_tensor(
                            out=hT[:, j, sub * 512:(sub + 1) * 512], in0=sgt, in1=ps[:, bu, :], op=ALU.mult),
                            reads=[("sg", ss), ("ps", bu), ("pace", p, n)], writes=[("hT", j, sub)])
                    n += 1
                if T + 1 < NTT:
                    for sub in range(2):
                        load_xT(p, (T + 1) * 2 + sub, xT[sub], sub)
                else:
                    load_xT(p + 1, 0, xT[0], 0)
                for s_ in range(8):
                    st = T * 8 + s_
                    ln.proj(st, NJ, lambda kc, s_=s_: hT[:, kc, s_ * 128:(s_ + 1) * 128],
                            lambda kc, h: W2[:, kc, h * 512:(h + 1) * 512],
                            reads=[("hT", j, q_) for j in range(NJ) for q_ in range(2)] + ["W2"])
            pending_ln.append(ln)

        def phase_odd(p, L, i):
            A.reset()
            pb = 32 + i * 136
            spt = A.alloc([8, 512], F32)
            cst = [A.alloc([8, 512], F32) for _ in range(2)]
            t32 = A.alloc([8, 512], F32)
            khml = A.alloc([8, 3, 512], BF16)
            ngt = A.alloc([8, 512], BF16)
            ones3 = A.alloc([8, 3, 512], BF16)
            ones8f = A.alloc([8, 512], F32)
            mark2 = A.off
            Win = A.alloc([128, 8, OD_IN], BF16)
            dgw = A.alloc([128, 4, 31, 128], BF16)
            onesm = A.alloc([128, 128], BF16)
            xT = xTg
            hb = [A.alloc([128, 4, 544], BF16) for _ in range(2)]
            hc = A.alloc([128, 4, 512], F32)
            hcb = [A.alloc([128, 512], BF16) for _ in range(2)]
            hsq = [A.alloc([128, 512], BF16) for _ in range(2)]
            sig = [A.alloc([128, 512], F32) for _ in range(2)]
            qst1 = A.alloc([128, 4, 512], BF16)
            qst = [qst1, qst1]
            kst1 = A.alloc([128, 4, 512], BF16)
            kst = [kst1, kst1]
            vst1 = A.alloc([128, 4, 512], BF16)
            vst = [vst1, vst1]
            ydst = [A.alloc([128, 4, 512], BF16) for _ in range(2)]
            mean_sb = A.alloc([128, 512], F32)
            rstd = A.alloc([128, 512], F32)
            tmpn = [A.alloc([128, 512], F32) for _ in range(2)]
            etmp = A.alloc([8, 512], F32)

            load_xT(p, 0, xT[0], 0)
            for (c0, c1) in [(1536, 2568), (0, 512), (512, 1024), (1024, 1536)]:
                P.add("sp", lambda e, c0=c0, c1=c1: e.dma_start(
                    out=Win[:, :, c0:c1], in_=Wodin[i][:, c0:c1].rearrange("(kc p) n -> p kc n", p=128)),
                    reads=[("Wodin", i)], writes=[("Win", min(c0 // 512, 3))], sem="w_a%d" % min(c0 // 512, 3))
            P.add("dve", lambda e: e.memset(onesm, 1.0 / 512.0), writes=["onesm"])
            P.add("dve", lambda e: e.memset(ones8f, 1.0), writes=["ones8f"])
            P.add("dve", lambda e: e.memset(ones3, 1.0), writes=["ones3"])
            P.add("dve", lambda e: e.memset(hb[0][:, :, 0:30], 0.0), writes=[("hbh", 0, c) for c in range(4)])
            def build_dgw(c):
                for k in range(31):
                    wcol = cpar[:, pb + c * 31 + k:pb + c * 31 + k + 1]
                    P.add("dve", lambda e, c=c, k=k, wcol=wcol: e.tensor_scalar(
                        out=dgw[:, c, k, :], in0=ident, scalar1=wcol, scalar2=None, op0=ALU.mult),
                        reads=["cbf", "cpar"], writes=[("dgw", c)])

            NT = S // 512
            yield

            def odd_tile(t):
                cur, prev = t % 2, 1 - (t % 2)
                if t + 1 < NT:
                    load_xT(p, t + 1, xT[prev], prev)

                def inproj(col, m=128):
                    b = psA.next()
                    mm_group(ps[0:m, b, :], [(Win[:, kc, col:col + m], xT[cur][:, kc, :]) for kc in range(8)],
                             reads=[("Win", min(col // 512, 3)), ("xT", cur)], writes=[("ps", b)])
                    return b

                b = inproj(1536, 8)
                P.add("act", lambda e, b=b: e.activation(out=etmp, in_=ps[0:8, b, :], func=AF.Exp, bias=nfb[0:8, i:i + 1], scale=-1.0),
                      reads=[("ps", b), "nfb"], writes=["etmp"])
                P.add("act", lambda e: e.activation(out=spt, in_=etmp, func=AF.Ln, bias=1.0, scale=1.0),
                      reads=["etmp"], writes=["spt"])
                cs = cst[cur]
                init = 0.0 if t == 0 else cst[prev][:, 511:512]
                P.add("dve", lambda e: e.tensor_tensor_scan(out=cs, data0=ones8f, data1=spt, initial=init, op0=ALU.mult, op1=ALU.add),
                      reads=["ones8f", "spt", ("cst", prev)], writes=[("cst", cur)])
                P.add("dve", lambda e: e.tensor_copy(out=khml[:, 0, :], in_=cs), reads=[("cst", cur)], writes=["k_hi"])
                P.add("dve", lambda e: e.tensor_copy(out=t32, in_=khml[:, 0, :]), reads=["k_hi"], writes=["t32"])
                P.add("dve", lambda e: e.tensor_scalar(out=ngt, in0=t32, scalar1=-1.0, scalar2=None, op0=ALU.mult), reads=["t32"], writes=["ngt"])
                P.add("dve", lambda e: e.tensor_tensor(out=t32, in0=cs, in1=t32, op=ALU.subtract), reads=[("cst", cur), "t32"], writes=["t32"])
                P.add("dve", lambda e: e.tensor_copy(out=khml[:, 1, :], in_=t32), reads=["t32"], writes=["k_mid"])
                P.add("dve", lambda e: e.tensor_copy(out=spt, in_=khml[:, 1, :]), reads=["k_mid", "spt"], writes=["spt"])
                P.add("dve", lambda e: e.tensor_tensor(out=t32, in0=t32, in1=spt, op=ALU.subtract), reads=["t32", "spt"], writes=["t32"])
                P.add("dve", lambda e: e.tensor_copy(out=khml[:, 2, :], in_=t32), reads=["t32"], writes=["k_lo"])
                gx = Grp("s_qx")
                cols = slice(t * 512, (t + 1) * 512)
                P.add("sp", lambda e: e.dma_start(out=KX[:, 0:3, cols], in_=khml), reads=["k_hi", "k_mid", "k_lo"], writes=[("KX", t)], grp=gx)
                P.add("sp", lambda e: e.dma_start(out=KX[:, 3, cols], in_=ones3[:, 0, :]), reads=["ones3"], writes=[("KX", t)], grp=gx)
                P.add("sp", lambda e: e.dma_start(out=QX[:, 0:3, cols], in_=ones3), reads=["ones3"], writes=[("QX", t)], grp=gx)
                P.add("sp", lambda e: e.dma_start(out=QX[:, 3, cols], in_=ngt), reads=["ngt"], writes=[("QX", t)], grp=gx)
                for c in range(4):
                    b = inproj(2056 + c * 128)
                    sgt = sig[c % 2]
                    P.add("act", lambda e, b=b, sgt=sgt: e.activation(out=sgt, in_=ps[:, b, :], func=AF.Sigmoid),
                          reads=[("ps", b)], writes=[("sig", c % 2)])
                    b = inproj(1544 + c * 128)
                    P.add("dve", lambda e, b=b, sgt=sgt, c=c: e.tensor_tensor(out=hb[cur][:, c, 30:542], in0=sgt, in1=ps[:, b, :],
                                                                            op=ALU.mult),
                          reads=[("sig", c % 2), ("ps", b)], writes=[("hbm", cur, c)])
                    if t > 0:
                        P.add("act", lambda e, c=c: e.activation(out=hb[cur][:, c, 0:30], in_=hb[prev][:, c, 512:542], func=AF.Copy),
                              reads=[("hbm", prev, c)], writes=[("hbh", cur, c)])
                yield
                def do_q():
                    for c in range(4):
                        b = inproj(c * 128)
                        P.add("act", lambda e, b=b, c=c: e.activation(out=qst[cur][:, c, :], in_=ps[:, b, :], func=AF.Identity, scale=0.125),
                              reads=[("ps", b)], writes=[("qst", 0, c)])
                    P.add("sp", lambda e: e.dma_start(out=QT[:, t * 512:(t + 1) * 512].rearrange("(c p) n -> p c n", p=128),
                                                      in_=qst[cur]),
                          reads=[("qst", 0, c_) for c_ in range(4)], writes=[("QT", t)], sem="s_q0")

                def do_k():
                    for c in range(4):
                        b = inproj(512 + c * 128)
                        P.add("dve", lambda e, b=b, c=c: e.tensor_copy(out=kst[cur][:, c, :], in_=ps[:, b, :]),
                              reads=[("ps", b)], writes=[("kst", 0, c)])
                    P.add("sp", lambda e: e.dma_start(out=KT[:, t * 512:(t + 1) * 512].rearrange("(c p) n -> p c n", p=128),
                                                      in_=kst[cur]),
                          reads=[("kst", 0, c_) for c_ in range(4)], writes=[("KT", t)], sem="s_k0")

                def do_v(ss):
                    for s_ in ss:
                        b = psA.next()
                        mm_group(ps[:, b, :], [(xT[cur][:, kc, s_ * 128:(s_ + 1) * 128], Win[:, kc, 1024:1536]) for kc in range(8)],
                                 reads=[("Win", 2), ("xT", cur)], writes=[("ps", b)])
                        P.add("act", lambda e, b=b, s_=s_: e.activation(out=vst[cur][:, s_, :], in_=ps[:, b, :], func=AF.Copy),
                              reads=[("ps", b)], writes=[("vst", 0, s_)])
                    if 3 in ss:
                        P.add("sp", lambda e: e.dma_start(out=Vd[t * 512:(t + 1) * 512, :].rearrange("(s p) n -> p s n", p=128),
                                                          in_=vst[cur]),
                              reads=[("vst", 0, q_) for q_ in range(4)], writes=[("Vd", t)], sem="s_v0")

                def do_conv(c):
                    b = psC.next()
                    mm_group(ps[:, b, :], [(dgw[:, c, k, :], hb[cur][:, c, k:k + 512]) for k in range(31)],
                             reads=[("dgw", c), ("hbm", cur, c), ("hbh", cur, c)], writes=[("ps", b)])
                    bias = cpar[:, pb + 124 + c:pb + 125 + c]
                    P.add("act", lambda e, b=b, c=c, bias=bias: e.activation(out=hc[:, c, :], in_=ps[:, b, :], func=AF.Identity,
                                                                             bias=bias, scale=1.0),
                          reads=[("ps", b), "cpar"], writes=[("hc", c)])
                    P.add("pool", lambda e, c=c: e.tensor_copy(out=hcb[c % 2], in_=hc[:, c, :]), reads=[("hc", c)],
                          writes=[("hcb", c % 2)])
                    P.add("dve", lambda e, c=c: e.tensor_tensor(out=hsq[c % 2], in0=hc[:, c, :], in1=hc[:, c, :], op=ALU.mult),
                          reads=[("hc", c)], writes=[("hsq", c % 2)])

                def do_stats(c):
                    P.add("pe", lambda e, c=c: e.matmul(ps[:, 6, :], lhsT=onesm, rhs=hcb[c % 2], start=(c == 0), stop=(c == 3)),
                          reads=["onesm", ("hcb", c % 2)], writes=[("ps", 6)])
                    P.add("pe", lambda e, c=c: e.matmul(ps[:, 7, :], lhsT=onesm, rhs=hsq[c % 2], start=(c == 0), stop=(c == 3)),
                          reads=["onesm", ("hsq", c % 2)], writes=[("ps", 7)])

                if t == 0:
                    build_dgw(0)
                do_q()
                if t == 0:
                    build_dgw(1)
                do_conv(0)
                do_k()
                if t == 0:
                    build_dgw(2)
                do_conv(1)
                do_stats(0)
                do_v([0, 1])
                if t == 0:
                    build_dgw(3)
                do_conv(2)
                do_stats(1)
                do_v([2, 3])
                do_conv(3)
                do_stats(2)
                do_stats(3)
                yield
                P.add("act", lambda e: e.activation(out=mean_sb, in_=ps[:, 6, :], func=AF.Copy), reads=[("ps", 6)], writes=["mean_sb"])
                P.add("dve", lambda e: e.tensor_tensor(out=rstd, in0=mean_sb, in1=mean_sb, op=ALU.mult), reads=["mean_sb"], writes=["rstd"])
                P.add("dve", lambda e: e.tensor_tensor(out=rstd, in0=ps[:, 7, :], in1=rstd, op=ALU.subtract),
                      reads=[("ps", 7), "rstd"], writes=["rstd"])
                P.add("act", lambda e: e.activation(out=rstd, in_=rstd, func=AF.Sqrt, bias=LN_EPS, scale=1.0), reads=["rstd"], writes=["rstd"])
                P.add("dve", lambda e: e.reciprocal(out=rstd, in_=rstd), reads=["rstd"], writes=["rstd"])
                for c in range(4):
                    tn = tmpn[c % 2]
                    P.add("dve", lambda e, c=c, tn=tn: e.tensor_tensor(out=tn, in0=hc[:, c, :], in1=mean_sb, op=ALU.subtract),
                          reads=[("hc", c), "mean_sb"], writes=[("tmpn", c % 2)])
                    P.add("dve", lambda e, tn=tn: e.tensor_tensor(out=tn, in0=tn, in1=rstd, op=ALU.mult),
                          reads=[("tmpn", c % 2), "rstd"], writes=[("tmpn", c % 2)])
                    gsc = cpar[:, pb + 128 + c:pb + 129 + c]
                    bsc = cpar[:, pb + 132 + c:pb + 133 + c]
                    P.add("act", lambda e, c=c, tn=tn, gsc=gsc, bsc=bsc: e.activation(out=ydst[cur][:, c, :], in_=tn, func=AF.Silu,
                                                                                      bias=bsc, scale=gsc),
                          reads=[("tmpn", c % 2), "cpar"], writes=[("ydst", cur, c)])
                P.add("sp", lambda e, t=t: e.dma_start(out=YD[:, t * 512:(t + 1) * 512].rearrange("(c p) n -> p c n", p=128),
                                                       in_=ydst[cur]),
                      reads=[("ydst", cur, c_) for c_ in range(4)], writes=[("YD", t)], sem="s_y%d" % cur)

            gens = [odd_tile(t) for t in range(NT)]
            next(gens[0])
            next(gens[0])
            for t in range(NT):
                bgcast(2)
                if t + 1 < NT:
                    next(gens[t + 1])
                next(gens[t], None)
                if t + 1 < NT:
                    next(gens[t + 1])

            P.barrier()
            A.reset()
            qa = [A.alloc([128, S], BF16) for _ in range(2)]
            ka = [A.alloc([128, S], BF16) for _ in range(2)]
            va = [A.alloc([128, 32, 128], BF16) for _ in range(2)]
            pbuf = [A.alloc([128, 512], BF16) for _ in range(3)]
            rbuf = [A.alloc([64, 512], F32) for _ in range(2)]
            ost = [A.alloc([64, 512], BF16) for _ in range(2)]
            for s_ in range(2):
                P.add("dve", lambda e, s_=s_: e.memset(va[s_][:, :, 64:128], 1.0), writes=[("va1", s_)])
            Wout = A.alloc([128, 8, D], BF16)
            mixT = [A.alloc([128, 8, 512], BF16) for _ in range(2)]
            lnbox = []
            mix_done = set()

            def load_mix(t):
                if t in mix_done or t >= NT:
                    return
                mix_done.add(t)
                s_ = t % 2
                gm = Grp("m_x%d" % s_)
                P.add("sp", lambda e: e.dma_start(out=mixT[s_][:, 0:4, :],
                                                  in_=YC[:, t * 512:(t + 1) * 512].rearrange("(c p) n -> p c n", p=128)),
                      reads=[("YC", t)], writes=[("mixT", s_)], grp=gm)
                P.add("sp", lambda e: e.dma_start(out=mixT[s_][:, 4:8, :],
                                                  in_=YD[:, t * 512:(t + 1) * 512].rearrange("(c p) n -> p c n", p=128)),
                      reads=[("YD", t)], writes=[("mixT", s_)], grp=gm)

            def load_head(h):
                s_ = h % 2
                gq_ = Grp("a_q%d" % s_)
                P.add("sp", lambda e: e.dma_start(out=qa[s_][0:64, :], in_=QT[h * 64:(h + 1) * 64, :]),
                      reads=[("QT", t) for t in range(NT)], writes=[("qa", s_)], grp=gq_)
                P.add("sp", lambda e: e.dma_start(out=qa[s_][64:68, :], in_=QX[h]), reads=[("QX", t) for t in range(NT)], writes=[("qa", s_)], grp=gq_)
                gk_ = Grp("a_k%d" % s_)
                P.add("sp", lambda e: e.dma_start(out=ka[s_][0:64, :], in_=KT[h * 64:(h + 1) * 64, :]),
                      reads=[("KT", t) for t in range(NT)], writes=[("ka", s_)], grp=gk_)
                P.add("sp", lambda e: e.dma_start(out=ka[s_][64:68, :], in_=KX[h]), reads=[("KX", t) for t in range(NT)], writes=[("ka", s_)], grp=gk_)
                gv_ = Grp("a_v%d" % s_)
                for q4 in range(4):
                    P.add("sp", lambda e, q4=q4: e.dma_start(
                        out=va[s_][:, q4 * 8:(q4 + 1) * 8, 0:64],
                        in_=Vd[q4 * 1024:(q4 + 1) * 1024, h * 64:(h + 1) * 64].rearrange("(kb p) d -> p kb d", p=128)),
                        reads=[("Vd", t) for t in range(NT)], writes=[("va", s_)], grp=gv_)

            units = []
            for h in range(8):
                for c in range(8):
                    nkb = 4 * c + 4
                    for kb in range(nkb):
                        units.append((h, c, kb, nkb))
            psS = Rot([0, 1, 2])
            psO = Rot([4, 5])
            sbank = {}
            obank = {}

            def emit_S(u):
                h, c, kb, nkb = units[u]
                s_ = h % 2
                j = kb - 4 * c
                q0 = max(0, j) * 128
                b = psS.next()
                sbank[u] = b

                def fn(pe, b=b, s_=s_, c=c, kb=kb, j=j, q0=q0):
                    ins = pe.matmul(ps[:, b, q0:512], lhsT=ka[s_][0:68, kb * 128:(kb + 1) * 128],
                                    rhs=qa[s_][0:68, c * 512 + q0:(c + 1) * 512], start=True, stop=(j < 0))
                    if j >= 0:
                        ins = pe.matmul(ps[:, b, q0:q0 + 128], lhsT=ident, rhs=tri, start=False, stop=True)
                    return ins
                P.add("pe", fn, reads=[("ka", s_), ("qa", s_), "cbf"], writes=[("ps", b)])
                pslot = u % 3
                P.add("act", lambda e, b=b, q0=q0, pslot=pslot: e.activation(out=pbuf[pslot][:, q0:512], in_=ps[:, b, q0:512], func=AF.Exp),
                      reads=[("ps", b)], writes=[("pbuf", pslot)])

            def emit_PV(u):
                h, c, kb, nkb = units[u]
                s_ = h % 2
                j = kb - 4 * c
                q0 = max(0, j) * 128
                if kb == 0:
                    obank[(h, c)] = psO.next()
                ob = obank[(h, c)]
                pslot = u % 3
                P.add("pe", lambda pe, ob=ob, s_=s_, kb=kb, q0=q0, pslot=pslot, nkb=nkb: pe.matmul(
                    ps[:, ob, q0:512], lhsT=va[s_][:, kb, :], rhs=pbuf[pslot][:, q0:512], start=(kb == 0), stop=(kb == nkb - 1)),
                    reads=[("va", s_), ("va1", s_), ("pbuf", pslot)], writes=[("ps", ob)])
                if kb == nkb - 1:
                    os_ = (h * 8 + c) % 2
                    P.add("dve", lambda e, ob=ob, os_=os_: e.reciprocal(out=rbuf[os_], in_=ps[64:128, ob, :]),
                          reads=[("ps", ob)], writes=[("rbuf", os_)])
                    P.add("dve", lambda e, ob=ob, os_=os_: e.tensor_tensor(out=ost[os_], in0=ps[0:64, ob, :], in1=rbuf[os_], op=ALU.mult),
                          reads=[("ps", ob), ("rbuf", os_)], writes=[("ost", os_)])
                    P.add("sp", lambda e, h=h, c=c, os_=os_: e.dma_start(out=YC[h * 64:(h + 1) * 64, c * 512:(c + 1) * 512], in_=ost[os_]),
                          reads=[("ost", os_)], writes=[("YC", c)], sem="s_o%d" % os_)
                    if c == 7 and h + 2 < 8:
                        load_head(h + 2)
                    if h == 7 and c <= 1:
                        load_mix(c)
                        if c == 0:
                            lnbox[0].prefetch(0)

            nu = len(units)
            load_head(0)
            load_head(1)
            ln = make_ln(p, ln_mix_g, ln_mix_b, L)
            lnbox.append(ln)
            P.add("sp", lambda e: e.dma_start(out=Wout, in_=Wodout[i].rearrange("(kc p) n -> p kc n", p=128)),
                  reads=[("Wodout", i)], writes=["Wout"], sem="w_b")
            emit_S(0)
            emit_S(1)
            for u in range(nu):
                if u + 2 < nu:
                    emit_S(u + 2)
                emit_PV(u)

            load_mix(0)
            ln.prefetch(0)
            for t in range(NT):
                if t + 1 < NT:
                    load_mix(t + 1)
                if t == 3:
                    xit = load_xT_iter(p + 1, 0, xTg[0], 0)
                cur = t % 2
                for s_ in range(4):
                    st = t * 4 + s_
                    if t >= 3:
                        next(xit, None)
                        next(xit, None)
                    ln.proj(st, 8, lambda kc, s_=s_: mixT[cur][:, kc, s_ * 128:(s_ + 1) * 128],
                            lambda kc, h: Wout[:, kc, h * 512:(h + 1) * 512],
                            reads=[("mixT", cur), "Wout"])
            pending_ln.append(ln)

        def flush_pending():
            while pending_ln:
                pending_ln.pop(0).flush()

        for p, (kind, L, i) in enumerate(phases):
            bg_flush(p)
            if p > 0:
                P.barrier()
                cast_phase(p + 1)
            if kind == "E":
                gen = phase_even2(p, L, i)
            elif kind == "F":
                gen = phase_ffn(p, L)
            else:
                gen = phase_odd(p, L, i)
            next(gen)
            flush_pending()
            for _ in gen:
                pass
        flush_pending()

        P.add("sp", None, reads=[("y", st) for st in range(S // 128)])

        P.finalize()
        sems = {}
        for name in P.sem_names:
            sems[name] = es.enter_context(nc.semaphore("sem_" + name))
        for k, v in P.final_counts.items():
            assert v < 60000, (k, v)
        block = es.enter_context(nc.Block())

        @block.tensor
        def _(e):
            P.emit("pe", e, sems)

        @block.scalar
        def _(e):
            P.emit("act", e, sems)

        @block.vector
        def _(e):
            P.emit("dve", e, sems)

        @block.gpsimd
        def _(e):
            P.emit("pool", e, sems)

        @block.sync
        def _(e):
            P.emit("sp", e, sems)

    return nc


def _host_consts(inputs):
    cpar = np.zeros((128, NPAR), np.float32)
    for i in range(2):
        cw = np.asarray(inputs["ev_conv_w"][i], np.float32)
        pscale = np.asarray(inputs["ev_pool_scale"][i], np.float32)
        base = i * 16
        for c in range(4):
            cpar[:, base + c * 3:base + c * 3 + 3] = cw[:, c * 128:(c + 1) * 128].T
            cpar[:, base + 12 + c] = pscale[c * 128:(c + 1) * 128]
        dw = np.asarray(inputs["od_dw_w"][i], np.float32)
        base = 32 + i * 136
        for c in range(4):
            cpar[:, base + c * 31:base + (c + 1) * 31] = dw[:, c * 128:(c + 1) * 128].T
            cpar[:, base + 124 + c] = np.asarray(inputs["od_dw_b"][i], np.float32)[c * 128:(c + 1) * 128]
            cpar[:, base + 128 + c] = np.asarray(inputs["od_cn_g"][i], np.float32)[c * 128:(c + 1) * 128]
            cpar[:, base + 132 + c] = np.asarray(inputs["od_cn_b"][i], np.float32)[c * 128:(c + 1) * 128]
        cpar[0:8, 304 + i] = np.asarray(inputs["od_forget_b"][i], np.float32)
    cpar[:, 306:322] = (1.0 / np.arange(1, 17, dtype=np.float32))[None, :]
    cbf = np.zeros((128, 256), np.float32)
    cbf[:, 0:128] = np.eye(128, dtype=np.float32)
    kk = np.arange(128)[:, None]
    qq = np.arange(128)[None, :]
    cbf[:, 128:256] = np.where(kk > qq, NEG, 0.0)
    return cpar, cbf.astype(ml_dtypes.bfloat16)


_NC_CACHE = {}


def kernel(**inputs):
    nph = int(inputs.pop("_nphases", 8))
    dbg = bool(inputs.pop("_debug", False))
    if (nph, dbg) not in _NC_CACHE:
        _NC_CACHE[(nph, dbg)] = build_nc(nph, dbg)
    nc = _NC_CACHE[(nph, dbg)]
    cpar, cbf = _host_consts(inputs)
    x = np.ascontiguousarray(np.asarray(inputs["x"], np.float32))
    shared = {k: np.ascontiguousarray(np.asarray(inputs[k], np.float32)) for k in
              ("ev_w_in", "ev_pool_w", "ev_w_out", "od_w_in", "od_w_out", "ln_mix_g", "ln_mix_b",
               "ln_ffn_g", "ln_ffn_b", "ffn_w_in", "ffn_w_out")}
    shared["cpar"] = cpar
    shared["cbf"] = cbf
    ncores = x.shape[0]
    in_maps = []
    for b in range(ncores):
        m = dict(shared)
        m["x"] = x[b]
        in_maps.append(m)
    res = run_bass_kernel_spmd(nc, in_maps, core_ids=list(range(ncores)))
    if dbg:
        return res.results
    return np.stack([np.asarray(r["y"], np.float32) for r in res.results], axis=0)
```

```python
import numpy as np
import ml_dtypes
import concourse.bass as bass
import concourse.mybir as mybir
from concourse.bass_utils import run_bass_kernel_spmd

F32 = mybir.dt.float32
BF16 = mybir.dt.bfloat16
U8 = mybir.dt.uint8
AF = mybir.ActivationFunctionType
ALU = mybir.AluOpType

S = 4096
D = 1024
DEPTH = 4
DFF = 2816
NJ = DFF // 128
OD_IN = 2568
ALPHA = (2.0 * DEPTH) ** 0.25
LN_EPS = 1e-5
NPAR = 324
NEG = -30000.0


class Op:
    __slots__ = ("eng", "fn", "dma", "sem", "deps", "sig", "cnt", "gend", "grp")


class Grp:
    def __init__(self, sem):
        self.sem = sem
        self.ops = []


class Prog:
    COMPUTE = ("pe", "act", "dve", "pool")

    def __init__(self):
        self.ops = []
        self.lw = {}
        self.rd = {}
        self.last_eng = {}
        self.last_sem = {}
        self.bar_deps = []
        self.bar_pending = set()

    def add(self, eng, fn, reads=(), writes=(), grp=None, sem=None):
        op = Op()
        op.eng = eng
        op.fn = fn
        if sem is not None and grp is None:
            grp = Grp(sem)
        op.dma = grp is not None
        op.grp = grp
        op.sig = False
        op.cnt = 0
        deps = {}
        for k in reads:
            w = self.lw.get(k)
            if w is not None:
                deps[w] = True
        for k in writes:
            w = self.lw.get(k)
            if w is not None:
                deps.setdefault(w, False)
            for r in self.rd.get(k, ()):
                deps.setdefault(r, False)
        if eng in self.bar_pending:
            for d in self.bar_deps:
                deps[d] = True
            self.bar_pending.discard(eng)
        deps.pop(op, None)
        keep = []
        for d, raw in deps.items():
            if d.dma and op.dma and d.grp is grp:
                continue
            if (not d.dma) and (not op.dma) and d.eng == eng:
                if eng == "pe":
                    continue
            keep.append(d)
            if not d.dma:
                d.sig = True
        op.deps = keep
        for k in reads:
            self.rd.setdefault(k, []).append(op)
        for k in writes:
            self.lw[k] = op
            self.rd[k] = []
        self.ops.append(op)
        if op.dma:
            grp.ops.append(op)
            self.last_sem[grp.sem] = op
        else:
            self.last_eng[eng] = op
        return op

    def barrier(self):
        self.bar_deps = list(self.last_eng.values()) + list(self.last_sem.values())
        self.bar_pending = set(["pe", "act", "dve", "pool", "sp"])

    def finalize(self):
        cnt = {}
        for op in self.ops:
            if op.dma:
                s = op.grp.sem
                cnt[s] = cnt.get(s, 0) + 16
                op.cnt = cnt[s]
            elif op.sig:
                cnt[op.eng] = cnt.get(op.eng, 0) + 1
                op.cnt = cnt[op.eng]
        self.sem_names = sorted(cnt.keys())
        self.final_counts = cnt
        for op in self.ops:
            w = {}
            for d in op.deps:
                if d.dma:
                    s = d.grp.sem
                    v = d.grp.ops[-1].cnt
                else:
                    s = d.eng
                    v = d.cnt
                if w.get(s, 0) < v:
                    w[s] = v
            op.deps = w

    def emit(self, engname, eng, sems):
        known = {}
        n = 0
        for op in self.ops:
            if op.eng != engname:
                continue
            for s, v in op.deps.items():
                if known.get(s, 0) < v:
                    eng.wait_ge(sems[s], v)
                    known[s] = v
            if op.fn is None:
                continue
            ins = op.fn(eng)
            n += 1
            if op.dma:
                ins.then_inc(sems[op.grp.sem], 16)
            elif op.sig:
                ins.then_inc(sems[engname], 1)
        return n


class Rot:
    def __init__(self, items):
        self.items = list(items)
        self.i = 0

    def next(self):
        v = self.items[self.i % len(self.items)]
        self.i += 1
        return v


class Arena:
    def __init__(self, ap, size):
        self.ap = ap
        self.size = size
        self.off = 0
        self.mark = 0

    def alloc(self, shape, dt, parts=128):
        esz = 4 if dt == F32 else 2
        n = 1
        for s_ in shape[1:]:
            n *= s_
        nbytes = n * esz
        off = (self.off + 63) // 64 * 64
        assert off + nbytes <= self.size, f"arena overflow {off + nbytes} > {self.size}"
        self.off = off + nbytes
        v = self.ap[0:shape[0], off:off + nbytes].bitcast(dt)
        if len(shape) == 3:
            v = v.rearrange("p (a b) -> p a b", a=shape[1])
        elif len(shape) == 4:
            v = v.rearrange("p (a b c) -> p a b c", a=shape[1], b=shape[2])
        return v

    def set_mark(self):
        self.mark = self.off

    def reset(self):
        self.off = self.mark


def build_nc(nphases=8, debug=False):
    nc = bass.Bass("TRN2", target_bir_lowering=False)
    P = Prog()

    def dram_in(name, shape, dt=F32):
        return nc.dram_tensor(name, list(shape), dt, kind="ExternalInput").ap()

    x_in = dram_in("x", [S, D])
    ev_w_in = dram_in("ev_w_in", [2, D, 2048])
    ev_pool_w = dram_in("ev_pool_w", [2, 4, 128, 128])
    ev_w_out = dram_in("ev_w_out", [2, D, D])
    od_w_in = dram_in("od_w_in", [2, D, OD_IN])
    od_w_out = dram_in("od_w_out", [2, D, D])
    ln_mix_g = dram_in("ln_mix_g", [4, D])
    ln_mix_b = dram_in("ln_mix_b", [4, D])
    ln_ffn_g = dram_in("ln_ffn_g", [4, D])
    ln_ffn_b = dram_in("ln_ffn_b", [4, D])
    ffn_w_in = dram_in("ffn_w_in", [4, D, 2 * DFF])
    ffn_w_out = dram_in("ffn_w_out", [4, DFF, D])
    cpar_in = dram_in("cpar", [128, NPAR])
    cbf_in = dram_in("cbf", [128, 256], BF16)
    y_out = nc.dram_tensor("y", [S, D], F32, kind="ExternalOutput").ap()

    def scratch(name, shape, dt):
        if debug and name in ("QT", "KT", "Vd", "QX", "KX", "YC", "YD"):
            return nc.dram_tensor(name, list(shape), dt, kind="ExternalOutput").ap()
        return nc.dram_tensor(name, list(shape), dt).ap()

    X = [scratch(f"Xs{i}", [S, D], F32) for i in range(2)]
    XB = [scratch(f"XBs{i}", [S, D], BF16) for i in range(2)]
    Wevin = [scratch(f"Wevin{i}", [D, 2048], BF16) for i in range(2)]
    Wevout = [scratch(f"Wevout{i}", [D, D], BF16) for i in range(2)]
    Wpool = [scratch(f"Wpool{i}", [4, 128, 128], BF16) for i in range(2)]
    Wodin = [scratch(f"Wodin{i}", [D, OD_IN], BF16) for i in range(2)]
    Wodout = [scratch(f"Wodout{i}", [D, D], BF16) for i in range(2)]
    W1b = [scratch(f"W1b{l}", [D, 2 * DFF], BF16) for l in range(4)]
    W2b = [scratch(f"W2b{l}", [DFF, D], BF16) for l in range(4)]
    QT = scratch("QT", [512, S], BF16)
    KT = scratch("KT", [512, S], BF16)
    Vd = scratch("Vd", [S, 512], BF16)
    QX = scratch("QX", [8, 4, S], BF16)
    KX = scratch("KX", [8, 4, S], BF16)
    YC = scratch("YC", [512, S], BF16)
    YD = scratch("YD", [512, S], BF16)

    phases = [("E", 0, 0), ("F", 0, 0), ("O", 1, 0), ("F", 1, 0),
              ("E", 2, 1), ("F", 2, 1), ("O", 3, 1), ("F", 3, 1)][:nphases]

    ARENA = 212480
    import contextlib
    with contextlib.ExitStack() as es:
        arena_t = es.enter_context(nc.sbuf_tensor("arena", [128, ARENA], U8))
        ps = es.enter_context(nc.psum_tensor("ps", [128, 8, 512], F32))
        A = Arena(arena_t, ARENA)

        cbf = A.alloc([128, 256], BF16)
        ident = cbf[:, 0:128]
        tri = cbf[:, 128:256]
        cpar = A.alloc([128, NPAR], F32)
        nfb = A.alloc([128, 2], F32)
        lng2 = [A.alloc([128, D], F32) for _ in range(2)]
        lnb2 = [A.alloc([128, D], F32) for _ in range(2)]
        xres = [A.alloc([128, D], F32) for _ in range(4)]
        xbo = [A.alloc([128, D], BF16) for _ in range(2)]
        stt_ = [A.alloc([128, 2, 6], F32) for _ in range(4)]
        mvs = [A.alloc([128, 8], F32) for _ in range(4)]
        xTg = [A.alloc([128, 8, 512], BF16) for _ in range(2)]
        A.set_mark()

        P.add("sp", lambda e: e.dma_start(out=cbf, in_=cbf_in), writes=["cbf"], sem="c0")
        P.add("sp", lambda e: e.dma_start(out=cpar, in_=cpar_in), writes=["cpar"], sem="c1")
        P.add("dve", lambda e: e.tensor_scalar(out=nfb[:, 0:2], in0=cpar[:, 304:306], scalar1=-1.0, scalar2=None,
                                               op0=ALU.mult), reads=["cpar"], writes=["nfb"])

        bgq = []
        bg_mode = [False]
        bg_tag = [0]

        bg_pace = [None]

        def cast(out_ap, in_ap, key, sem):
            def emit():
                wk = [key] + ([bg_pace[0]] if bg_pace[0] is not None else [])
                P.add("pool", lambda e: e.dma_start(out=out_ap, in_=in_ap), writes=wk, grp=sem)
            if bg_mode[0]:
                bgq.append((bg_tag[0], emit))
            else:
                emit()

        def bgcast(n, pace=None):
            bg_pace[0] = pace
            for _ in range(n):
                if bgq:
                    bgq.pop(0)[1]()
            bg_pace[0] = None

        def bg_flush(ptag):
            while bgq and bgq[0][0] <= ptag:
                bgq.pop(0)[1]()

        def cast_x(t):
            gxb = Grp("k_xb%d" % t)
            P.add("pool", lambda e, t=t: e.dma_start(out=XB[0][t * 512:(t + 1) * 512, :], in_=x_in[t * 512:(t + 1) * 512, :]),
                  writes=[("XB", 0, 4 * t + q) for q in range(4)], grp=gxb)

        def cast_even(i):
            g = Grp("k_ev%d" % i)
            for q in (1, 2, 0, 3):
                cast(Wevin[i][:, q * 512:(q + 1) * 512], ev_w_in[i, :, q * 512:(q + 1) * 512], ("Wevin", i, q), Grp("k_ev%d_%d" % (i, q)))
            for r in range(2):
                cast(Wevout[i][r * 512:(r + 1) * 512, :], ev_w_out[i, r * 512:(r + 1) * 512, :], ("Wevout", i), g)
            cast(Wpool[i].rearrange("g c d -> (g c) d"), ev_pool_w[i].rearrange("g c d -> (g c) d"), ("Wpool", i), g)

        def cast_odd(i):
            g = Grp("k_od%d" % i)
            for r in range(4):
                cast(Wodin[i][r * 256:(r + 1) * 256, :], od_w_in[i, r * 256:(r + 1) * 256, :], ("Wodin", i), g)
            for r in range(2):
                cast(Wodout[i][r * 512:(r + 1) * 512, :], od_w_out[i, r * 512:(r + 1) * 512, :], ("Wodout", i), g)

        def cast_ffn(l):
            g = Grp("k_f%d" % l)
            for r in range(8):
                cast(W1b[l][r * 128:(r + 1) * 128, :], ffn_w_in[l, r * 128:(r + 1) * 128, :], ("W1b", l), g)
            for r in range(4):
                cast(W2b[l][r * 704:(r + 1) * 704, :], ffn_w_out[l, r * 704:(r + 1) * 704, :], ("W2b", l), g)

        def cast_phase(p):
            if p >= len(phases):
                return
            bg_tag[0] = p
            kind, L, i = phases[p]
            if kind == "E":
                cast_even(i)
            elif kind == "O":
                cast_odd(i)
            else:
                cast_ffn(L)

        cast_x(0)
        cast_phase(0)
        cast_x(1)
        cast_x(2)
        bg_mode[0] = True
        cast_phase(1)

        pending_ln = []
        psA = Rot([0, 1, 2, 3])
        psB = Rot([4, 5, 6, 7])
        psC = Rot([4, 5])
        lnslot = Rot([0, 1, 2])
        xboslot = Rot([0, 1])

        def mm_group(out_ap, pairs, reads, writes):
            def fn(pe, pairs=pairs, out_ap=out_ap):
                n = len(pairs)
                ins = None
                for q, (l, r) in enumerate(pairs):
                    ins = pe.matmul(out_ap, lhsT=l, rhs=r, start=(q == 0), stop=(q == n - 1))
                return ins
            return P.add("pe", fn, reads=reads, writes=writes)

        class LNStage:
            def __init__(self, Xsrc, xkey, Xdst, dkey, XBdst, bkey, g_ap, b_ap, final, par):
                self.Xsrc, self.xkey, self.Xdst, self.dkey = Xsrc, xkey, Xdst, dkey
                self.XBdst, self.bkey, self.final = XBdst, bkey, final
                self.pref = {}
                self.pipe = []
                self.psB = psB
                lng, lnb = lng2[par], lnb2[par]
                self.lng, self.lnb = lng, lnb
                self.gk, self.bk = ("lng", par), ("lnb", par)
                P.add("sp", lambda e: e.dma_start(out=lng, in_=g_ap.to_broadcast([128, D])), writes=[self.gk], sem="l_g%d" % par)
                P.add("sp", lambda e: e.dma_start(out=lnb, in_=b_ap.to_broadcast([128, D])), writes=[self.bk], sem="l_b%d" % par)

            def prefetch(self, st):
                if st in self.pref or st >= S // 128:
                    return
                slot = st % 4
                self.pref[st] = slot
                xr = xres[slot]
                src = self.Xsrc[st * 128:(st + 1) * 128, :]
                P.add("sp", lambda e: e.dma_start(out=xr, in_=src), reads=[(self.xkey, st)],
                      writes=[("xres", slot, 0), ("xres", slot, 1)], sem="l_x%d" % slot)

            def tick(self):
                ready3 = [ent for ent in self.pipe if ent[0] is None and ent[1] is not None]
                for ent in self.pipe:
                    if ent[0] is not None:
                        ent[0]()
                        ent[0] = None
                for ent in ready3:
                    ent[1]()
                    ent[1] = None
                self.pipe = [e_ for e_ in self.pipe if e_[1] is not None]

            def flush(self):
                while self.pipe:
                    self.tick()

            def run(self, st, banks):
                self.prefetch(st)
                slot = self.pref.pop(st)
                xr = xres[slot]
                sa = stt_[slot]
                mv = mvs[slot]
                xk = [("xres", slot, 0), ("xres", slot, 1)]
                for h in range(2):
                    P.add("dve", lambda e, h=h: e.scalar_tensor_tensor(
                        out=xr[:, h * 512:(h + 1) * 512], in0=xr[:, h * 512:(h + 1) * 512], scalar=ALPHA,
                        in1=ps[:, banks[h], :], op0=ALU.mult, op1=ALU.add),
                        reads=[("xres", slot, h), ("ps", banks[h])], writes=[("xres", slot, h)])
                    P.add("dve", lambda e, h=h: e.bn_stats(out=sa[:, h, :], in_=xr[:, h * 512:(h + 1) * 512]),
                          reads=[("xres", slot, h)], writes=[("st", slot, h)])
                P.add("dve", lambda e: e.bn_aggr(out=mv[:, 0:2], in_=sa), reads=[("st", slot, 0), ("st", slot, 1)],
                      writes=[("mv", slot, 0)])
                self.prefetch(st + 1)

                def stage2():
                    P.add("act", lambda e: e.activation(out=mv[:, 2:3], in_=mv[:, 1:2], func=AF.Sqrt, bias=LN_EPS, scale=1.0),
                          reads=[("mv", slot, 0)], writes=[("mv", slot, 1)])
                    P.add("dve", lambda e: e.reciprocal(out=mv[:, 3:4], in_=mv[:, 2:3]), reads=[("mv", slot, 1)],
                          writes=[("mv", slot, 2)])
                    P.add("dve", lambda e: e.tensor_scalar(out=xr, in0=xr, scalar1=mv[:, 0:1], scalar2=mv[:, 3:4],
                                                           op0=ALU.subtract, op1=ALU.mult),
                          reads=xk + [("mv", slot, 0), ("mv", slot, 2)], writes=xk)
                    P.add("pool", lambda e: e.tensor_tensor(out=xr, in0=xr, in1=self.lng, op=ALU.mult), reads=xk + [self.gk], writes=xk)
                    P.add("pool", lambda e: e.tensor_tensor(out=xr, in0=xr, in1=self.lnb, op=ALU.add), reads=xk + [self.bk], writes=xk)

                def stage3():
                    dst = self.Xdst[st * 128:(st + 1) * 128, :]
                    if not self.final:
                        bs = st % 2
                        xb_ = xbo[bs]
                        P.add("act", lambda e: e.activation(out=xb_, in_=xr, func=AF.Copy), reads=xk, writes=[("xbo", bs)])
                        P.add("sp", lambda e: e.dma_start(out=dst, in_=xr), reads=xk, writes=[(self.dkey, st)], sem="s_x%d" % slot)
                        bdst = self.XBdst[st * 128:(st + 1) * 128, :]
                        P.add("sp", lambda e: e.dma_start(out=bdst, in_=xb_), reads=[("xbo", bs)],
                              writes=[self.bkey + (st,)], sem="s_b%d" % bs)
                    else:
                        P.add("sp", lambda e: e.dma_start(out=dst, in_=xr), reads=xk, writes=[(self.dkey, st)], sem="s_x%d" % slot)

                self.pipe.append([stage2, stage3])

            def proj(self, st, nk, act_fn, w_fn, reads):
                self.tick()
                banks = [self.psB.next(), self.psB.next()]
                for h in range(2):
                    mm_group(ps[:, banks[h], :], [(act_fn(kc), w_fn(kc, h)) for kc in range(nk)],
                             reads=reads, writes=[("ps", banks[h])])
                self.run(st, banks)

        def make_ln(p, g_all, b_all, L):
            final = (p == len(phases) - 1)
            if p == 0:
                Xsrc, xkey = x_in, "x_in"
            else:
                Xsrc, xkey = X[p % 2], ("X", p % 2)
            if final:
                Xdst, dkey = y_out, "y"
            else:
                Xdst, dkey = X[(p + 1) % 2], ("X", (p + 1) % 2)
            return LNStage(Xsrc, xkey, Xdst, dkey, XB[(p + 1) % 2], ("XB", (p + 1) % 2), g_all[L:L + 1, :], b_all[L:L + 1, :], final, p % 2)

        xt_done = set()

        def load_xT(p, t, dst, slot, q="sp"):
            for _ in load_xT_iter(p, t, dst, slot, q):
                pass

        def load_xT_iter(p, t, dst, slot, q="sp"):
            if p >= len(phases) or (p, t) in xt_done:
                return
            xt_done.add((p, t))
            g = Grp("xT%d" % slot)
            for kc in range(8):
                yield
                P.add(q, lambda e, kc=kc: e.dma_start_transpose(
                    out=dst[:, kc, :], in_=XB[p % 2][t * 512:(t + 1) * 512, kc * 128:(kc + 1) * 128]),
                    reads=[("XB", p % 2, 4 * t + q) for q in range(4)], writes=[("xT", slot)], grp=g)

        def phase_even2(p, L, i):
            A.reset()
            ln = make_ln(p, ln_mix_g, ln_mix_b, L)
            psA5 = Rot([0, 1, 2, 3, 7])
            ln.psB = Rot([4, 5, 6])
            Win = A.alloc([128, 8, 2048], BF16)
            Wout = A.alloc([128, 8, D], BF16)
            poolw = A.alloc([128, 4, 128], BF16)
            xT = xTg
            cv = [A.alloc([128, 4, 514], F32) for _ in range(2)]
            ub = [A.alloc([128, 4, 528], F32) for _ in range(2)]
            ctmp = [A.alloc([128, 512], F32) for _ in range(2)]
            acc = [A.alloc([128, 4, 512], F32) for _ in range(2)]
            tA = [A.alloc([128, 528], F32) for _ in range(2)]
            tB = [A.alloc([128, 528], F32) for _ in range(2)]
            fx = A.alloc([128, 16], F32)
            pooled = [A.alloc([128, 4, 512], BF16) for _ in range(2)]
            mixT = [A.alloc([128, 8, 512], BF16) for _ in range(2)]
            pb = i * 16

            load_xT(p, 0, xT[0], 0)
            for q in (1, 2, 0, 3):
                P.add("sp", lambda e, q=q: e.dma_start(
                    out=Win[:, :, q * 512:(q + 1) * 512],
                    in_=Wevin[i][:, q * 512:(q + 1) * 512].rearrange("(kc p) n -> p kc n", p=128)),
                    reads=[("Wevin", i, q)], writes=[("Win", q)], sem="w_a%d" % q)
            P.add("sp", lambda e: e.dma_start(out=Wout, in_=Wevout[i].rearrange("(kc p) n -> p kc n", p=128)),
                  reads=[("Wevout", i)], writes=["Wout"], sem="w_b")
            P.add("sp", lambda e: e.dma_start(out=poolw, in_=Wpool[i].rearrange("g c d -> c g d")),
                  reads=[("Wpool", i)], writes=["poolw"], sem="w_c")
            P.add("dve", lambda e: e.memset(cv[0][:, :, 0:2], 0.0), writes=[("cvh", 0, j) for j in range(4)])
            P.add("dve", lambda e: e.memset(ub[0][:, :, 0:16], 0.0), writes=[("ubh", 0, j) for j in range(4)])

            NT = S // 512
            yield

            def partA(t):
                cur, prev = t % 2, 1 - (t % 2)
                if t + 1 < NT:
                    load_xT(p, t + 1, xT[prev], prev)

                def inproj(col):
                    b = psA5.next()
                    mm_group(ps[:, b, :], [(Win[:, kc, col:col + 128], xT[cur][:, kc, :]) for kc in range(8)],
                             reads=[("Win", col // 512), ("xT", cur)], writes=[("ps", b)])
                    return b

                for j in range(4):
                    b = inproj(512 + j * 128)
                    ct = ctmp[j % 2]
                    P.add("act", lambda e, b=b, ct=ct: e.activation(out=ct, in_=ps[:, b, :], func=AF.Copy),
                          reads=[("ps", b)], writes=[("ctmp", j % 2)])
                    yield
                    b = inproj(1024 + j * 128)
                    P.add("dve", lambda e, b=b, ct=ct, j=j: e.tensor_tensor(out=cv[cur][:, j, 2:514], in0=ct, in1=ps[:, b, :],
                                                                          op=ALU.mult),
                          reads=[("ctmp", j % 2), ("ps", b)], writes=[("cvm", cur, j)])
                    if t > 0:
                        P.add("act", lambda e, j=j: e.activation(out=cv[cur][:, j, 0:2], in_=cv[prev][:, j, 512:514], func=AF.Copy),
                              reads=[("cvm", prev, j)], writes=[("cvh", cur, j)])
                    w0 = cpar[:, pb + j * 3 + 0:pb + j * 3 + 1]
                    w1 = cpar[:, pb + j * 3 + 1:pb + j * 3 + 2]
                    w2 = cpar[:, pb + j * 3 + 2:pb + j * 3 + 3]
                    ac = acc[cur][:, j, :]
                    P.add("act", lambda e, j=j, ac=ac, w2=w2: e.activation(out=ac, in_=cv[cur][:, j, 2:514], func=AF.Identity, scale=w2),
                          reads=[("cvm", cur, j), "cpar"], writes=[("acc", cur, j)])
                    P.add("dve", lambda e, j=j, ac=ac, w1=w1: e.scalar_tensor_tensor(
                        out=ac, in0=cv[cur][:, j, 1:513], scalar=w1, in1=ac, op0=ALU.mult, op1=ALU.add),
                        reads=[("cvm", cur, j), ("cvh", cur, j), ("acc", cur, j), "cpar"], writes=[("acc", cur, j)])
                    P.add("dve", lambda e, j=j, ac=ac, w0=w0: e.scalar_tensor_tensor(
                        out=ac, in0=cv[cur][:, j, 0:512], scalar=w0, in1=ac, op0=ALU.mult, op1=ALU.add),
                        reads=[("cvm", cur, j), ("cvh", cur, j), ("acc", cur, j), "cpar"], writes=[("acc", cur, j)])
                    yield
                for j in range(4):
                    b = inproj(j * 128)
                    P.add("dve", lambda e, j=j, b=b: e.tensor_tensor(out=mixT[cur][:, j, :], in0=acc[cur][:, j, :], in1=ps[:, b, :],
                                                                     op=ALU.mult),
                          reads=[("acc", cur, j), ("ps", b)], writes=[("mixT", cur, j)])
                    yield
                for gq in (3, 2, 1, 0):
                    win = 2 ** (gq + 1)
                    b = inproj(1536 + gq * 128)
                    U = ub[cur][:, gq, :]
                    P.add("act", lambda e, b=b, U=U: e.activation(out=U[:, 16:528], in_=ps[:, b, :], func=AF.Copy),
                          reads=[("ps", b)], writes=[("ubm", cur, gq)])
                    if t > 0:
                        P.add("act", lambda e, gq=gq, U=U: e.activation(out=U[:, 0:16], in_=ub[prev][:, gq, 512:528], func=AF.Copy),
                              reads=[("ubm", prev, gq)], writes=[("ubh", cur, gq)])
                    src = U
                    srck = [("ubm", cur, gq), ("ubh", cur, gq)]
                    for k in range(gq + 1):
                        d = 2 ** k
                        lo = 2 * d - 1
                        dst = (tA if k % 2 == 0 else tB)[gq % 2]
                        dk = [("tAB", k % 2, gq % 2)]
                        P.add("pool", lambda e, dst=dst, src=src, lo=lo, d=d: e.tensor_tensor(
                            out=dst[:, lo:528], in0=src[:, lo:528], in1=src[:, lo - d:528 - d], op=ALU.add),
                            reads=srck, writes=dk)
                        src, srck = dst, dk
                    pl = pooled[cur][:, gq, :]
                    pk = ("pooled", cur, gq)
                    P.add("dve", lambda e, pl=pl, src=src, U=U, win=win: e.scalar_tensor_tensor(
                        out=pl, in0=src[:, 16:528], scalar=1.0 / win, in1=U[:, 16:528], op0=ALU.mult, op1=ALU.subtract),
                        reads=srck + [("ubm", cur, gq)], writes=[pk])
                    if t == 0:
                        n1 = win - 1
                        P.add("dve", lambda e, src=src, n1=n1: e.tensor_tensor(
                            out=fx[:, 0:n1], in0=src[:, 16:16 + n1], in1=cpar[:, 306:306 + n1], op=ALU.mult),
                            reads=srck + ["cpar"], writes=["fx"])
                        P.add("dve", lambda e, pl=pl, U=U, n1=n1: e.tensor_tensor(
                            out=pl[:, 0:n1], in0=fx[:, 0:n1], in1=U[:, 16:16 + n1], op=ALU.subtract),
                            reads=["fx", ("ubm", cur, gq), pk], writes=[pk])
                    yield

            def partB(t):
                cur = t % 2
                for gq in range(4):
                    b = psA5.next()
                    mm_group(ps[:, b, :], [(poolw[:, gq, :], pooled[cur][:, gq, :])], reads=["poolw", ("pooled", cur, gq)],
                             writes=[("ps", b)])
                    sc = cpar[:, pb + 12 + gq:pb + 13 + gq]
                    P.add("act", lambda e, b=b, gq=gq, sc=sc: e.activation(out=mixT[cur][:, 4 + gq, :], in_=ps[:, b, :],
                                                                           func=AF.Identity, scale=sc),
                          reads=[("ps", b), "cpar"], writes=[("mixT", cur, 4 + gq)])
                yield
                for s_ in range(4):
                    st = t * 4 + s_
                    ln.proj(st, 8, lambda kc, s_=s_: mixT[cur][:, kc, s_ * 128:(s_ + 1) * 128],
                            lambda kc, h: Wout[:, kc, h * 512:(h + 1) * 512],
                            reads=[("mixT", cur, j) for j in range(8)] + ["Wout"])
                    yield

            def steps(g, n):
                if g is None:
                    return
                for _ in range(n):
                    next(g, None)

            ln.prefetch(0)
            for _ in partA(0):
                pass
            for t in range(NT):
                if p == 0 and t + 3 < NT:
                    cast_x(t + 3)
                bgcast(2)
                gA = partA(t + 1) if t + 1 < NT else None
                gB = partB(t)
                if t == NT - 1:
                    load_xT(p + 1, 0, xT[0], 0)
                steps(gA, 4)
                steps(gB, 1)
                steps(gA, 4)
                steps(gB, 1)
                steps(gA, 4)
                steps(gB, 1)
                steps(gA, 4)
                steps(gB, 2)
                for g_ in (gA, gB):
                    if g_ is not None:
                        for _ in g_:
                            pass
            pending_ln.append(ln)

        w1cnt = [0]

        def phase_ffn(p, L):
            A.reset()
            ln = make_ln(p, ln_ffn_g, ln_ffn_b, L)
            NR = 6
            W2 = A.alloc([128, NJ, D], BF16)
            w1r = [A.alloc([128, 8, 2, 128], BF16) for _ in range(NR)]
            xT = xTg
            hT = A.alloc([128, NJ, 1024], BF16)
            sg = [A.alloc([128, 512], F32) for _ in range(2)]
            sgslot = Rot([0, 1])

            NTT = S // 1024
            pieces = [(T, j) for T in range(NTT) for j in range(NJ)]

            def load_piece(n):
                if n >= len(pieces):
                    return
                T, j = pieces[n]
                slot = n % NR
                gp = Grp("w1r%d" % slot)
                for gu in range(2):
                    c0 = gu * DFF + j * 128
                    P.add("sp", lambda e, gu=gu, c0=c0: e.dma_start(
                        out=w1r[slot][:, :, gu, :], in_=W1b[L][:, c0:c0 + 128].rearrange("(kc p) n -> p kc n", p=128)),
                        reads=[("W1b", L)], writes=[("w1r", slot)], grp=gp)

            load_xT(p, 0, xT[0], 0)
            load_piece(0)
            load_xT(p, 1, xT[1], 1)
            for n in range(1, NR - 1):
                load_piece(n)
            def load_W2():
                g = Grp("w_a")
                for q in range(2):
                    P.add("sp", lambda e, q=q: e.dma_start(
                        out=W2[:, q * 11:(q + 1) * 11, :],
                        in_=W2b[L][q * 11 * 128:(q + 1) * 11 * 128, :].rearrange("(j p) n -> p j n", p=128)),
                        reads=[("W2b", L)], writes=["W2"], grp=g)
            ln.prefetch(0)
            yield

            n = 0
            for T in range(NTT):
                for j in range(NJ):
                    if j % 8 == 4:
                        bgcast(1, pace=("pace", p, n - 1))
                    load_piece(n + NR - 1)
                    if n == 6:
                        load_W2()
                    slot = n % NR
                    for sub in range(2):
                        bg, bu = psA.next(), psA.next()
                        mm_group(ps[:, bg, :], [(w1r[slot][:, kc, 0, :], xT[sub][:, kc, :]) for kc in range(8)],
                                 reads=[("w1r", slot), ("xT", sub)], writes=[("ps", bg)])
                        mm_group(ps[:, bu, :], [(w1r[slot][:, kc, 1, :], xT[sub][:, kc, :]) for kc in range(8)],
                                 reads=[("w1r", slot), ("xT", sub)], writes=[("ps", bu)])
                        ss = sgslot.next()
                        sgt = sg[ss]
                        P.add("act", lambda e, bg=bg, sgt=sgt: e.activation(out=sgt, in_=ps[:, bg, :], func=AF.Silu),
                              reads=[("ps", bg)], writes=[("sg", ss)])
                        P.add("dve", lambda e, bu=bu, sgt=sgt, j=j, sub=sub: e.tensor_tensor(
                            out=hT[:, j, sub * 512:(sub + 1) * 512], in0=sgt, in1=ps[:, bu, :], op=ALU.mult),
                            reads=[("sg", ss), ("ps", bu), ("pace", p, n)], writes=[("hT", j, sub)])
                    n += 1
                if T + 1 < NTT:
                    for sub in range(2):
                        load_xT(p, (T + 1) * 2 + sub, xT[sub], sub)
                else:
                    load_xT(p + 1, 0, xT[0], 0)
                for s_ in range(8):
                    st = T * 8 + s_
                    ln.proj(st, NJ, lambda kc, s_=s_: hT[:, kc, s_ * 128:(s_ + 1) * 128],
                            lambda kc, h: W2[:, kc, h * 512:(h + 1) * 512],
                            reads=[("hT", j, q_) for j in range(NJ) for q_ in range(2)] + ["W2"])
            pending_ln.append(ln)

        def phase_odd(p, L, i):
            A.reset()
            pb = 32 + i * 136
            spt = A.alloc([8, 512], F32)
            cst = [A.alloc([8, 512], F32) for _ in range(2)]
            t32 = A.alloc([8, 512], F32)
            khml = A.alloc([8, 3, 512], BF16)
            ngt = A.alloc([8, 512], BF16)
            ones3 = A.alloc([8, 3, 512], BF16)
            ones8f = A.alloc([8, 512], F32)
            mark2 = A.off
            Win = A.alloc([128, 8, OD_IN], BF16)
            dgw = A.alloc([128, 4, 31, 128], BF16)
            onesm = A.alloc([128, 128], BF16)
            xT = xTg
            hb = [A.alloc([128, 4, 544], BF16) for _ in range(2)]
            hc = A.alloc([128, 4, 512], F32)
            hcb = [A.alloc([128, 512], BF16) for _ in range(2)]
            hsq = [A.alloc([128, 512], BF16) for _ in range(2)]
            sig = [A.alloc([128, 512], F32) for _ in range(2)]
            qst1 = A.alloc([128, 4, 512], BF16)
            qst = [qst1, qst1]
            kst1 = A.alloc([128, 4, 512], BF16)
            kst = [kst1, kst1]
            vst1 = A.alloc([128, 4, 512], BF16)
            vst = [vst1, vst1]
            ydst = [A.alloc([128, 4, 512], BF16) for _ in range(2)]
            mean_sb = A.alloc([128, 512], F32)
            rstd = A.alloc([128, 512], F32)
            tmpn = [A.alloc([128, 512], F32) for _ in range(2)]
            etmp = A.alloc([8, 512], F32)

            load_xT(p, 0, xT[0], 0)
            for (c0, c1) in [(1536, 2568), (0, 512), (512, 1024), (1024, 1536)]:
                P.add("sp", lambda e, c0=c0, c1=c1: e.dma_start(
                    out=Win[:, :, c0:c1], in_=Wodin[i][:, c0:c1].rearrange("(kc p) n -> p kc n", p=128)),
                    reads=[("Wodin", i)], writes=[("Win", min(c0 // 512, 3))], sem="w_a%d" % min(c0 // 512, 3))
            P.add("dve", lambda e: e.memset(onesm, 1.0 / 512.0), writes=["onesm"])
            P.add("dve", lambda e: e.memset(ones8f, 1.0), writes=["ones8f"])
            P.add("dve", lambda e: e.memset(ones3, 1.0), writes=["ones3"])
            P.add("dve", lambda e: e.memset(hb[0][:, :, 0:30], 0.0), writes=[("hbh", 0, c) for c in range(4)])
            def build_dgw(c):
                for k in range(31):
                    wcol = cpar[:, pb + c * 31 + k:pb + c * 31 + k + 1]
                    P.add("dve", lambda e, c=c, k=k, wcol=wcol: e.tensor_scalar(
                        out=dgw[:, c, k, :], in0=ident, scalar1=wcol, scalar2=None, op0=ALU.mult),
                        reads=["cbf", "cpar"], writes=[("dgw", c)])

            NT = S // 512
            yield

            def odd_tile(t):
                cur, prev = t % 2, 1 - (t % 2)
                if t + 1 < NT:
                    load_xT(p, t + 1, xT[prev], prev)

                def inproj(col, m=128):
                    b = psA.next()
                    mm_group(ps[0:m, b, :], [(Win[:, kc, col:col + m], xT[cur][:, kc, :]) for kc in range(8)],
                             reads=[("Win", min(col // 512, 3)), ("xT", cur)], writes=[("ps", b)])
                    return b

                b = inproj(1536, 8)
                P.add("act", lambda e, b=b: e.activation(out=etmp, in_=ps[0:8, b, :], func=AF.Exp, bias=nfb[0:8, i:i + 1], scale=-1.0),
                      reads=[("ps", b), "nfb"], writes=["etmp"])
                P.add("act", lambda e: e.activation(out=spt, in_=etmp, func=AF.Ln, bias=1.0, scale=1.0),
                      reads=["etmp"], writes=["spt"])
                cs = cst[cur]
                init = 0.0 if t == 0 else cst[prev][:, 511:512]
                P.add("dve", lambda e: e.tensor_tensor_scan(out=cs, data0=ones8f, data1=spt, initial=init, op0=ALU.mult, op1=ALU.add),
                      reads=["ones8f", "spt", ("cst", prev)], writes=[("cst", cur)])
                P.add("dve", lambda e: e.tensor_copy(out=khml[:, 0, :], in_=cs), reads=[("cst", cur)], writes=["k_hi"])
                P.add("dve", lambda e: e.tensor_copy(out=t32, in_=khml[:, 0, :]), reads=["k_hi"], writes=["t32"])
                P.add("dve", lambda e: e.tensor_scalar(out=ngt, in0=t32, scalar1=-1.0, scalar2=None, op0=ALU.mult), reads=["t32"], writes=["ngt"])
                P.add("dve", lambda e: e.tensor_tensor(out=t32, in0=cs, in1=t32, op=ALU.subtract), reads=[("cst", cur), "t32"], writes=["t32"])
                P.add("dve", lambda e: e.tensor_copy(out=khml[:, 1, :], in_=t32), reads=["t32"], writes=["k_mid"])
                P.add("dve", lambda e: e.tensor_copy(out=spt, in_=khml[:, 1, :]), reads=["k_mid", "spt"], writes=["spt"])
                P.add("dve", lambda e: e.tensor_tensor(out=t32, in0=t32, in1=spt, op=ALU.subtract), reads=["t32", "spt"], writes=["t32"])
                P.add("dve", lambda e: e.tensor_copy(out=khml[:, 2, :], in_=t32), reads=["t32"], writes=["k_lo"])
                gx = Grp("s_qx")
                cols = slice(t * 512, (t + 1) * 512)
                P.add("sp", lambda e: e.dma_start(out=KX[:, 0:3, cols], in_=khml), reads=["k_hi", "k_mid", "k_lo"], writes=[("KX", t)], grp=gx)
                P.add("sp", lambda e: e.dma_start(out=KX[:, 3, cols], in_=ones3[:, 0, :]), reads=["ones3"], writes=[("KX", t)], grp=gx)
                P.add("sp", lambda e: e.dma_start(out=QX[:, 0:3, cols], in_=ones3), reads=["ones3"], writes=[("QX", t)], grp=gx)
                P.add("sp", lambda e: e.dma_start(out=QX[:, 3, cols], in_=ngt), reads=["ngt"], writes=[("QX", t)], grp=gx)
                for c in range(4):
                    b = inproj(2056 + c * 128)
                    sgt = sig[c % 2]
                    P.add("act", lambda e, b=b, sgt=sgt: e.activation(out=sgt, in_=ps[:, b, :], func=AF.Sigmoid),
                          reads=[("ps", b)], writes=[("sig", c % 2)])
                    b = inproj(1544 + c * 128)
                    P.add("dve", lambda e, b=b, sgt=sgt, c=c: e.tensor_tensor(out=hb[cur][:, c, 30:542], in0=sgt, in1=ps[:, b, :],
                                                                            op=ALU.mult),
                          reads=[("sig", c % 2), ("ps", b)], writes=[("hbm", cur, c)])
                    if t > 0:
                        P.add("act", lambda e, c=c: e.activation(out=hb[cur][:, c, 0:30], in_=hb[prev][:, c, 512:542], func=AF.Copy),
                              reads=[("hbm", prev, c)], writes=[("hbh", cur, c)])
                yield
                def do_q():
                    for c in range(4):
                        b = inproj(c * 128)
                        P.add("act", lambda e, b=b, c=c: e.activation(out=qst[cur][:, c, :], in_=ps[:, b, :], func=AF.Identity, scale=0.125),
                              reads=[("ps", b)], writes=[("qst", 0, c)])
                    P.add("sp", lambda e: e.dma_start(out=QT[:, t * 512:(t + 1) * 512].rearrange("(c p) n -> p c n", p=128),
                                                      in_=qst[cur]),
                          reads=[("qst", 0, c_) for c_ in range(4)], writes=[("QT", t)], sem="s_q0")

                def do_k():
                    for c in range(4):
                        b = inproj(512 + c * 128)
                        P.add("dve", lambda e, b=b, c=c: e.tensor_copy(out=kst[cur][:, c, :], in_=ps[:, b, :]),
                              reads=[("ps", b)], writes=[("kst", 0, c)])
                    P.add("sp", lambda e: e.dma_start(out=KT[:, t * 512:(t + 1) * 512].rearrange("(c p) n -> p c n", p=128),
                                                      in_=kst[cur]),
                          reads=[("kst", 0, c_) for c_ in range(4)], writes=[("KT", t)], sem="s_k0")

                def do_v(ss):
                    for s_ in ss:
                        b = psA.next()
                        mm_group(ps[:, b, :], [(xT[cur][:, kc, s_ * 128:(s_ + 1) * 128], Win[:, kc, 1024:1536]) for kc in range(8)],
                                 reads=[("Win", 2), ("xT", cur)], writes=[("ps", b)])
                        P.add("act", lambda e, b=b, s_=s_: e.activation(out=vst[cur][:, s_, :], in_=ps[:, b, :], func=AF.Copy),
                              reads=[("ps", b)], writes=[("vst", 0, s_)])
                    if 3 in ss:
                        P.add("sp", lambda e: e.dma_start(out=Vd[t * 512:(t + 1) * 512, :].rearrange("(s p) n -> p s n", p=128),
                                                          in_=vst[cur]),
                              reads=[("vst", 0, q_) for q_ in range(4)], writes=[("Vd", t)], sem="s_v0")

                def do_conv(c):
                    b = psC.next()
                    mm_group(ps[:, b, :], [(dgw[:, c, k, :], hb[cur][:, c, k:k + 512]) for k in range(31)],
                             reads=[("dgw", c), ("hbm", cur, c), ("hbh", cur, c)], writes=[("ps", b)])
                    bias = cpar[:, pb + 124 + c:pb + 125 + c]
                    P.add("act", lambda e, b=b, c=c, bias=bias: e.activation(out=hc[:, c, :], in_=ps[:, b, :], func=AF.Identity,
                                                                             bias=bias, scale=1.0),
                          reads=[("ps", b), "cpar"], writes=[("hc", c)])
                    P.add("pool", lambda e, c=c: e.tensor_copy(out=hcb[c % 2], in_=hc[:, c, :]), reads=[("hc", c)],
                          writes=[("hcb", c % 2)])
                    P.add("dve", lambda e, c=c: e.tensor_tensor(out=hsq[c % 2], in0=hc[:, c, :], in1=hc[:, c, :], op=ALU.mult),
                          reads=[("hc", c)], writes=[("hsq", c % 2)])

                def do_stats(c):
                    P.add("pe", lambda e, c=c: e.matmul(ps[:, 6, :], lhsT=onesm, rhs=hcb[c % 2], start=(c == 0), stop=(c == 3)),
                          reads=["onesm", ("hcb", c % 2)], writes=[("ps", 6)])
                    P.add("pe", lambda e, c=c: e.matmul(ps[:, 7, :], lhsT=onesm, rhs=hsq[c % 2], start=(c == 0), stop=(c == 3)),
                          reads=["onesm", ("hsq", c % 2)], writes=[("ps", 7)])

                if t == 0:
                    build_dgw(0)
                do_q()
                if t == 0:
                    build_dgw(1)
                do_conv(0)
                do_k()
                if t == 0:
                    build_dgw(2)
                do_conv(1)
                do_stats(0)
                do_v([0, 1])
                if t == 0:
                    build_dgw(3)
                do_conv(2)
                do_stats(1)
                do_v([2, 3])
                do_conv(3)
                do_stats(2)
                do_stats(3)
                yield
                P.add("act", lambda e: e.activation(out=mean_sb, in_=ps[:, 6, :], func=AF.Copy), reads=[("ps", 6)], writes=["mean_sb"])
                P.add("dve", lambda e: e.tensor_tensor(out=rstd, in0=mean_sb, in1=mean_sb, op=ALU.mult), reads=["mean_sb"], writes=["rstd"])
                P.add("dve", lambda e: e.tensor_tensor(out=rstd, in0=ps[:, 7, :], in1=rstd, op=ALU.subtract),
                      reads=[("ps", 7), "rstd"], writes=["rstd"])
                P.add("act", lambda e: e.activation(out=rstd, in_=rstd, func=AF.Sqrt, bias=LN_EPS, scale=1.0), reads=["rstd"], writes=["rstd"])
                P.add("dve", lambda e: e.reciprocal(out=rstd, in_=rstd), reads=["rstd"], writes=["rstd"])
                for c in range(4):
                    tn = tmpn[c % 2]
                    P.add("dve", lambda e, c=c, tn=tn: e.tensor_tensor(out=tn, in0=hc[:, c, :], in1=mean_sb, op=ALU.subtract),
                          reads=[("hc", c), "mean_sb"], writes=[("tmpn", c % 2)])
                    P.add("dve", lambda e, tn=tn: e.tensor_tensor(out=tn, in0=tn, in1=rstd, op=ALU.mult),
                          reads=[("tmpn", c % 2), "rstd"], writes=[("tmpn", c % 2)])
                    gsc = cpar[:, pb + 128 + c:pb + 129 + c]
                    bsc = cpar[:, pb + 132 + c:pb + 133 + c]
                    P.add("act", lambda e, c=c, tn=tn, gsc=gsc, bsc=bsc: e.activation(out=ydst[cur][:, c, :], in_=tn, func=AF.Silu,
                                                                                      bias=bsc, scale=gsc),
                          reads=[("tmpn", c % 2), "cpar"], writes=[("ydst", cur, c)])
                P.add("sp", lambda e, t=t: e.dma_start(out=YD[:, t * 512:(t + 1) * 512].rearrange("(c p) n -> p c n", p=128),
                                                       in_=ydst[cur]),
                      reads=[("ydst", cur, c_) for c_ in range(4)], writes=[("YD", t)], sem="s_y%d" % cur)

            gens = [odd_tile(t) for t in range(NT)]
            next(gens[0])
            next(gens[0])
            for t in range(NT):
                bgcast(2)
                if t + 1 < NT:
                    next(gens[t + 1])
                next(gens[t], None)
                if t + 1 < NT:
                    next(gens[t + 1])

            P.barrier()
            A.reset()
            qa = [A.alloc([128, S], BF16) for _ in range(2)]
            ka = [A.alloc([128, S], BF16) for _ in range(2)]
            va = [A.alloc([128, 32, 128], BF16) for _ in range(2)]
            pbuf = [A.alloc([128, 512], BF16) for _ in range(4)]
            rbuf = [A.alloc([64, 512], F32) for _ in range(2)]
            ost = [A.alloc([64, 512], BF16) for _ in range(2)]
            for s_ in range(2):
                P.add("dve", lambda e, s_=s_: e.memset(va[s_][:, :, 64:128], 1.0), writes=[("va1", s_)])
            Wout = A.alloc([128, 8, D], BF16)
            mixT = [A.alloc([128, 8, 512], BF16) for _ in range(2)]
            lnbox = []
            mix_done = set()

            def load_mix(t):
                if t in mix_done or t >= NT:
                    return
                mix_done.add(t)
                s_ = t % 2
                gm = Grp("m_x%d" % s_)
                P.add("sp", lambda e: e.dma_start(out=mixT[s_][:, 0:4, :],
                                                  in_=YC[:, t * 512:(t + 1) * 512].rearrange("(c p) n -> p c n", p=128)),
                      reads=[("YC", t)], writes=[("mixT", s_)], grp=gm)
                P.add("sp", lambda e: e.dma_start(out=mixT[s_][:, 4:8, :],
                                                  in_=YD[:, t * 512:(t + 1) * 512].rearrange("(c p) n -> p c n", p=128)),
                      reads=[("YD", t)], writes=[("mixT", s_)], grp=gm)

            def load_head(h):
                s_ = h % 2
                gq_ = Grp("a_q%d" % s_)
                P.add("sp", lambda e: e.dma_start(out=qa[s_][0:64, :], in_=QT[h * 64:(h + 1) * 64, :]),
                      reads=[("QT", t) for t in range(NT)], writes=[("qa", s_)], grp=gq_)
                P.add("sp", lambda e: e.dma_start(out=qa[s_][64:68, :], in_=QX[h]), reads=[("QX", t) for t in range(NT)], writes=[("qa", s_)], grp=gq_)
                gk_ = Grp("a_k%d" % s_)
                P.add("sp", lambda e: e.dma_start(out=ka[s_][0:64, :], in_=KT[h * 64:(h + 1) * 64, :]),
                      reads=[("KT", t) for t in range(NT)], writes=[("ka", s_)], grp=gk_)
                P.add("sp", lambda e: e.dma_start(out=ka[s_][64:68, :], in_=KX[h]), reads=[("KX", t) for t in range(NT)], writes=[("ka", s_)], grp=gk_)
                gv_ = Grp("a_v%d" % s_)
                for q4 in range(4):
                    P.add("sp", lambda e, q4=q4: e.dma_start(
                        out=va[s_][:, q4 * 8:(q4 + 1) * 8, 0:64],
                        in_=Vd[q4 * 1024:(q4 + 1) * 1024, h * 64:(h + 1) * 64].rearrange("(kb p) d -> p kb d", p=128)),
                        reads=[("Vd", t) for t in range(NT)], writes=[("va", s_)], grp=gv_)

            units = []
            for h in range(8):
                for c in range(8):
                    nkb = 4 * c + 4
                    for kb in range(nkb):
                        units.append((h, c, kb, nkb))
            psS = Rot([0, 1, 2, 3])
            psO = Rot([4, 5])
            sbank = {}
            obank = {}

            def emit_S(u):
                h, c, kb, nkb = units[u]
                s_ = h % 2
                j = kb - 4 * c
                q0 = max(0, j) * 128
                b = psS.next()
                sbank[u] = b

                def fn(pe, b=b, s_=s_, c=c, kb=kb, j=j, q0=q0):
                    ins = pe.matmul(ps[:, b, q0:512], lhsT=ka[s_][0:68, kb * 128:(kb + 1) * 128],
                                    rhs=qa[s_][0:68, c * 512 + q0:(c + 1) * 512], start=True, stop=(j < 0))
                    if j >= 0:
                        ins = pe.matmul(ps[:, b, q0:q0 + 128], lhsT=ident, rhs=tri, start=False, stop=True)
                    return ins
                P.add("pe", fn, reads=[("ka", s_), ("qa", s_), "cbf"], writes=[("ps", b)])
                pslot = u % 4
                P.add("act", lambda e, b=b, q0=q0, pslot=pslot: e.activation(out=pbuf[pslot][:, q0:512], in_=ps[:, b, q0:512], func=AF.Exp),
                      reads=[("ps", b)], writes=[("pbuf", pslot)])

            def emit_PV(u):
                h, c, kb, nkb = units[u]
                s_ = h % 2
                j = kb - 4 * c
                q0 = max(0, j) * 128
                if kb == 0:
                    obank[(h, c)] = psO.next()
                ob = obank[(h, c)]
                pslot = u % 4
                P.add("pe", lambda pe, ob=ob, s_=s_, kb=kb, q0=q0, pslot=pslot, nkb=nkb: pe.matmul(
                    ps[:, ob, q0:512], lhsT=va[s_][:, kb, :], rhs=pbuf[pslot][:, q0:512], start=(kb == 0), stop=(kb == nkb - 1)),
                    reads=[("va", s_), ("va1", s_), ("pbuf", pslot)], writes=[("ps", ob)])
                if kb == nkb - 1:
                    os_ = (h * 8 + c) % 2
                    P.add("dve", lambda e, ob=ob, os_=os_: e.reciprocal(out=rbuf[os_], in_=ps[64:128, ob, :]),
                          reads=[("ps", ob)], writes=[("rbuf", os_)])
                    P.add("dve", lambda e, ob=ob, os_=os_: e.tensor_tensor(out=ost[os_], in0=ps[0:64, ob, :], in1=rbuf[os_], op=ALU.mult),
                          reads=[("ps", ob), ("rbuf", os_)], writes=[("ost", os_)])
                    P.add("sp", lambda e, h=h, c=c, os_=os_: e.dma_start(out=YC[h * 64:(h + 1) * 64, c * 512:(c + 1) * 512], in_=ost[os_]),
                          reads=[("ost", os_)], writes=[("YC", c)], sem="s_o%d" % os_)
                    if c == 7 and h + 2 < 8:
                        load_head(h + 2)
                    if h == 7 and c <= 1:
                        load_mix(c)
                        if c == 0:
                            lnbox[0].prefetch(0)

            nu = len(units)
            load_head(0)
            load_head(1)
            ln = make_ln(p, ln_mix_g, ln_mix_b, L)
            lnbox.append(ln)
            P.add("sp", lambda e: e.dma_start(out=Wout, in_=Wodout[i].rearrange("(kc p) n -> p kc n", p=128)),
                  reads=[("Wodout", i)], writes=["Wout"], sem="w_b")
            emit_S(0)
            emit_S(1)
            emit_S(2)
            for u in range(nu):
                if u + 3 < nu:
                    emit_S(u + 3)
                emit_PV(u)

            load_mix(0)
            ln.prefetch(0)
            for t in range(NT):
                if t + 1 < NT:
                    load_mix(t + 1)
                if t == 3:
                    xit = load_xT_iter(p + 1, 0, xTg[0], 0)
                cur = t % 2
                for s_ in range(4):
                    st = t * 4 + s_
                    if t >= 3:
                        next(xit, None)
                        next(xit, None)
                    ln.proj(st, 8, lambda kc, s_=s_: mixT[cur][:, kc, s_ * 128:(s_ + 1) * 128],
                            lambda kc, h: Wout[:, kc, h * 512:(h + 1) * 512],
                            reads=[("mixT", cur), "Wout"])
            pending_ln.append(ln)

        def flush_pending():
            while pending_ln:
                pending_ln.pop(0).flush()

        for p, (kind, L, i) in enumerate(phases):
            bg_flush(p)
            if p > 0:
                P.barrier()
                cast_phase(p + 1)
            if kind == "E":
                gen = phase_even2(p, L, i)
            elif kind == "F":
                gen = phase_ffn(p, L)
            else:
                gen = phase_odd(p, L, i)
            next(gen)
            flush_pending()
            for _ in gen:
                pass
        flush_pending()

        P.add("sp", None, reads=[("y", st) for st in range(S // 128)])

        P.finalize()
        sems = {}
        for name in P.sem_names:
            sems[name] = es.enter_context(nc.semaphore("sem_" + name))
        for k, v in P.final_counts.items():
            assert v < 60000, (k, v)
        block = es.enter_context(nc.Block())

        @block.tensor
        def _(e):
            P.emit("pe", e, sems)

        @block.scalar
        def _(e):
            P.emit("act", e, sems)

        @block.vector
        def _(e):
            P.emit("dve", e, sems)

        @block.gpsimd
        def _(e):
            P.emit("pool", e, sems)

        @block.sync
        def _(e):
            P.emit("sp", e, sems)

    return nc


def _host_consts(inputs):
    cpar = np.zeros((128, NPAR), np.float32)
    for i in range(2):
        cw = np.asarray(inputs["ev_conv_w"][i], np.float32)
        pscale = np.asarray(inputs["ev_pool_scale"][i], np.float32)
        base = i * 16
        for c in range(4):
            cpar[:, base + c * 3:base + c * 3 + 3] = cw[:, c * 128:(c + 1) * 128].T
            cpar[:, base + 12 + c] = pscale[c * 128:(c + 1) * 128]
        dw = np.asarray(inputs["od_dw_w"][i], np.float32)
        base = 32 + i * 136
        for c in range(4):
            cpar[:, base + c * 31:base + (c + 1) * 31] = dw[:, c * 128:(c + 1) * 128].T
            cpar[:, base + 124 + c] = np.asarray(inputs["od_dw_b"][i], np.float32)[c * 128:(c + 1) * 128]
            cpar[:, base + 128 + c] = np.asarray(inputs["od_cn_g"][i], np.float32)[c * 128:(c + 1) * 128]
            cpar[:, base + 132 + c] = np.asarray(inputs["od_cn_b"][i], np.float32)[c * 128:(c + 1) * 128]
        cpar[0:8, 304 + i] = np.asarray(inputs["od_forget_b"][i], np.float32)
    cpar[:, 306:322] = (1.0 / np.arange(1, 17, dtype=np.float32))[None, :]
    cbf = np.zeros((128, 256), np.float32)
    cbf[:, 0:128] = np.eye(128, dtype=np.float32)
    kk = np.arange(128)[:, None]
    qq = np.arange(128)[None, :]
    cbf[:, 128:256] = np.where(kk > qq, NEG, 0.0)
    return cpar, cbf.astype(ml_dtypes.bfloat16)


_NC_CACHE = {}


def kernel(**inputs):
    nph = int(inputs.pop("_nphases", 8))
    dbg = bool(inputs.pop("_debug", False))
    if (nph, dbg) not in _NC_CACHE:
        _NC_CACHE[(nph, dbg)] = build_nc(nph, dbg)
    nc = _NC_CACHE[(nph, dbg)]
    cpar, cbf = _host_consts(inputs)
    x = np.ascontiguousarray(np.asarray(inputs["x"], np.float32))
    shared = {k: np.ascontiguousarray(np.asarray(inputs[k], np.float32)) for k in
              ("ev_w_in", "ev_pool_w", "ev_w_out", "od_w_in", "od_w_out", "ln_mix_g", "ln_mix_b",
               "ln_ffn_g", "ln_ffn_b", "ffn_w_in", "ffn_w_out")}
    shared["cpar"] = cpar
    shared["cbf"] = cbf
    ncores = x.shape[0]
    in_maps = []
    for b in range(ncores):
        m = dict(shared)
        m["x"] = x[b]
        in_maps.append(m)
    res = run_bass_kernel_spmd(nc, in_maps, core_ids=list(range(ncores)))
    if dbg:
        return res.results
    return np.stack([np.asarray(r["y"], np.float32) for r in res.results], axis=0)
```
